# Optimizing a Trainium2 kernel written in Bass

```python
import jax, jax.numpy as jnp
from jax import lax
import numpy as np

D_MODEL = 4096
BATCH = 4
SEQ = 4096
DEPTH = 1
DEC_BATCH = 16
DEC_SEQ = 32
PAST_LEN = 4096

CHUNK = 64
SCAN_BLOCK = CHUNK // 4
D_MIX = D_MODEL
D_CONV = D_MIX // 2
CONV_GROUPS = 16
CONV_W = 3
D_HG = D_MIX - D_CONV
HG_DK = 128
HG_DV = 128
HG_HEADS = D_HG // HG_DK
N_IN = 3 * D_CONV + 4 * D_HG
D_FF = ((8 * D_MODEL // 3 + 255) // 256) * 256
NORM_EPS = 1e-6

kernel_name = "hybrid_shortconv_hgrn2_stream_step"


def rmsnorm(x, w):
    xf = x.astype(jnp.float32)
    y = xf * lax.rsqrt(jnp.mean(xf * xf, axis=-1, keepdims=True) + NORM_EPS)
    return (y * w.astype(jnp.float32)).astype(x.dtype)


def _to_blocks(a, n_blk):
    b, t, h, e = a.shape
    a = jnp.pad(a, ((0, 0), (0, n_blk * SCAN_BLOCK - t), (0, 0), (0, 0)))
    return a.reshape(b, n_blk, SCAN_BLOCK, h, e).transpose(1, 0, 3, 2, 4)


def hgrn2_scan(q, k, v, g, s0):
    b, t = q.shape[0], q.shape[1]
    n_blk = -(-t // SCAN_BLOCK)
    qb, kb, vb, gb = (_to_blocks(a, n_blk) for a in (q, k, v, g))
    mask = jnp.tril(jnp.ones((SCAN_BLOCK, SCAN_BLOCK), jnp.float32))
    mid = SCAN_BLOCK // 2

    def step(s, blk):
        qc, kc, vc, gc = blk
        lc = jnp.cumsum(gc, axis=-2)
        ref = lc[..., mid - 1:mid, :]
        qd = qc * jnp.exp(lc - ref)
        kd = kc * jnp.exp(ref - lc)
        att = jnp.einsum('bhtk,bhsk->bhts', qd, kd) * mask
        o = (jnp.einsum('bhts,bhsv->bhtv', att, vc)
             + jnp.einsum('bhtk,bhkv->bhtv', qc * jnp.exp(lc), s))
        llast = lc[..., -1:, :]
        s = (jnp.exp(llast)[..., 0, :, None] * s
             + jnp.einsum('bhsk,bhsv->bhkv', kc * jnp.exp(llast - lc), vc))
        return s, o

    s_new, o = lax.scan(step, s0, (qb, kb, vb, gb))
    o = o.transpose(1, 0, 3, 2, 4).reshape(b, n_blk * SCAN_BLOCK, HG_HEADS, HG_DV)[:, :t]
    return o, s_new


def layer(x, conv_buf, s0, lb, norm_mix, w_in, conv_w, hg_norm, w_out,
          norm_ffn, w_gate, w_up, w_down):
    b, t, _ = x.shape
    h = rmsnorm(x, norm_mix)
    proj = h @ w_in
    cuts = [D_CONV, 2 * D_CONV, 3 * D_CONV, 3 * D_CONV + D_HG,
            3 * D_CONV + 2 * D_HG, 3 * D_CONV + 3 * D_HG]
    b_gate, c_gate, u, q, fz, i_in, og = jnp.split(proj, cuts, axis=-1)

    cu = c_gate * u
    up = jnp.concatenate([conv_buf.astype(cu.dtype), cu], axis=1)
    conv = (conv_w[0] * up[:, 0:t] + conv_w[1] * up[:, 1:t + 1]
            + conv_w[2] * up[:, 2:t + 2])
    y_a = b_gate * conv
    new_buf = up[:, -(CONV_W - 1):]

    fzf = fz.astype(jnp.float32)
    f = lb + (1.0 - lb) * jax.nn.sigmoid(fzf)
    g_log = jnp.log(f)
    k = (1.0 - lb) * jax.nn.sigmoid(-fzf)
    qf = jax.nn.silu(q.astype(jnp.float32))
    heads = lambda a: a.reshape(b, t, HG_HEADS, -1)
    o, s_new = hgrn2_scan(heads(qf), heads(k), heads(i_in.astype(jnp.float32)),
                          heads(g_log), s0.astype(jnp.float32))
    o = rmsnorm(o, hg_norm) * jax.nn.silu(heads(og.astype(jnp.float32)))
    y_b = o.reshape(b, t, D_HG).astype(x.dtype)

    x = x + jnp.concatenate([y_a, y_b], axis=-1) @ w_out
    h2 = rmsnorm(x, norm_ffn)
    x = x + (jax.nn.silu(h2 @ w_gate) * (h2 @ w_up)) @ w_down
    return x, new_buf, s_new


def setup_inputs(seed: int = 0) -> dict:
    key = jax.random.key(seed)
    ks = jax.random.split(key, 16)
    nrm = lambda k, shape, s: jax.random.normal(k, shape, jnp.float32) * s
    gain = lambda k, shape: 1.0 + nrm(k, shape, 0.01)
    return {
        "x_prompt": nrm(ks[0], (BATCH, SEQ, D_MODEL), 1.0),
        "x_sample": nrm(ks[1], (DEC_BATCH, DEC_SEQ, D_MODEL), 1.0),
        "cache_conv": nrm(ks[2], (DEPTH, DEC_BATCH, CONV_W - 1, D_CONV), 1.0),
        "state_hgrn": nrm(ks[3], (DEPTH, DEC_BATCH, HG_HEADS, HG_DK, HG_DV), 0.5),
        "norm_mix": gain(ks[4], (DEPTH, D_MODEL)),
        "w_in": nrm(ks[5], (DEPTH, D_MODEL, N_IN), D_MODEL ** -0.5),
        "conv_w": nrm(ks[6], (DEPTH, CONV_W, D_CONV), CONV_W ** -0.5),
        "lb_logits": nrm(ks[7], (DEPTH + 1, D_HG), 0.1),
        "hg_norm": gain(ks[8], (DEPTH, HG_DV)),
        "w_out": nrm(ks[9], (DEPTH, D_MIX, D_MODEL), D_MIX ** -0.5),
        "norm_ffn": gain(ks[10], (DEPTH, D_MODEL)),
        "w_gate": nrm(ks[11], (DEPTH, D_MODEL, D_FF), D_MODEL ** -0.5),
        "w_up": nrm(ks[12], (DEPTH, D_MODEL, D_FF), D_MODEL ** -0.5),
        "w_down": nrm(ks[13], (DEPTH, D_FF, D_MODEL), D_FF ** -0.5),
        "norm_final": gain(ks[14], (D_MODEL,)),
    }


def reference(x_prompt, x_sample, cache_conv, state_hgrn, norm_mix, w_in, conv_w,
              lb_logits, hg_norm, w_out, norm_ffn, w_gate, w_up, w_down, norm_final):
    lb_all = jnp.cumsum(jax.nn.softmax(lb_logits.astype(jnp.float32), axis=0), axis=0)
    xp, xs = x_prompt, x_sample
    bp = x_prompt.shape[0]
    conv_p, hg_p, conv_s, hg_s = [], [], [], []
    for l in range(DEPTH):
        params = (norm_mix[l], w_in[l], conv_w[l], hg_norm[l], w_out[l],
                  norm_ffn[l], w_gate[l], w_up[l], w_down[l])
        buf0 = jnp.zeros((bp, CONV_W - 1, D_CONV), xp.dtype)
        s0 = jnp.zeros((bp, HG_HEADS, HG_DK, HG_DV), jnp.float32)
        xp, cb_p, sp = layer(xp, buf0, s0, lb_all[l], *params)
        xs, cb_s, ss = layer(xs, cache_conv[l], state_hgrn[l], lb_all[l], *params)
        conv_p.append(cb_p)
        hg_p.append(sp.astype(x_prompt.dtype))
        conv_s.append(cb_s.astype(cache_conv.dtype))
        hg_s.append(ss.astype(state_hgrn.dtype))
    y_prompt = rmsnorm(xp, norm_final)
    y_sample = rmsnorm(xs, norm_final)
    return (y_prompt, y_sample, jnp.stack(conv_p), jnp.stack(hg_p),
            jnp.stack(conv_s), jnp.stack(hg_s))
```

```python
import math
import numpy as np
from contextlib import ExitStack
import concourse.bass as bass
import concourse.mybir as mybir
from concourse.bass_utils import run_bass_kernel_spmd

F32 = mybir.dt.float32
BF16 = mybir.dt.bfloat16
AF = mybir.ActivationFunctionType
ALU = mybir.AluOpType
EPS = 1e-6
RIDE_ON = True
RIDE_FFN = True
RIDE_DOWN = True
ENGS = ("pe", "act", "dve", "pool", "sp")


class Op:
    __slots__ = ("eng", "fn", "deps", "dma", "signal", "ticket", "sem", "idx")

    def __init__(self, eng, fn, dma):
        self.eng = eng
        self.fn = fn
        self.dma = dma
        self.deps = set()
        self.signal = False
        self.ticket = None
        self.sem = None
        self.idx = None


class Prog:
    def __init__(self, dry=False, n_dma_sems=12):
        self.dry = dry
        self.ops = []
        self.last_w = {}
        self.readers = {}
        self.n_dma_sems = n_dma_sems
        self.final_waits = []

    def op(self, eng, fn, reads=(), writes=(), dma=False, must_wait=False):
        if self.dry:
            return None
        o = Op(eng, fn, dma)
        o.idx = len(self.ops)
        deps = set()
        for r in reads:
            w = self.last_w.get(r)
            if w is not None:
                deps.add(w)
        for w_ in writes:
            w = self.last_w.get(w_)
            if w is not None:
                deps.add(w)
            rd = self.readers.get(w_)
            if rd:
                deps.update(rd[0].values())
                deps.update(rd[1])
        o.deps = deps
        for r in reads:
            rd = self.readers.get(r)
            if rd is None:
                rd = self.readers[r] = ({}, [])
            if dma:
                rd[1].append(o.idx)
            else:
                rd[0][eng] = o.idx
        for w_ in writes:
            self.last_w[w_] = o.idx
            self.readers[w_] = ({}, [])
        self.ops.append(o)
        if must_wait:
            self.final_waits.append(o.idx)
        return o

    def emit(self, block, sems):
        ops = self.ops

        def ordered_free(p, o):
            return p.eng == "pe" and o.eng == "pe" and not p.dma and not o.dma

        for o in ops:
            for d in o.deps:
                p = ops[d]
                if ordered_free(p, o):
                    continue
                p.signal = True
        for i in self.final_waits:
            ops[i].signal = True
        cnt = {e: 0 for e in ENGS}
        dma_rr = {e: 0 for e in ENGS}
        dma_cnt = {}
        dma_prev = {}
        for o in ops:
            if o.dma:
                si = dma_rr[o.eng] % self.n_dma_sems
                dma_rr[o.eng] += 1
                key = (o.eng, si)
                prev = dma_prev.get(key)
                if prev is not None:
                    o.deps.add(prev)
                dma_prev[key] = o.idx
                dma_cnt[key] = dma_cnt.get(key, 0) + 16
                o.sem = ("dma",) + key
                o.ticket = dma_cnt[key]
                o.signal = True
            elif o.signal:
                cnt[o.eng] += 1
                o.sem = o.eng
                o.ticket = cnt[o.eng]
        per_eng = {e: [o for o in ops if o.eng == e] for e in ENGS}
        final_waits = [ops[i] for i in self.final_waits]

        def run(eng_name, eng):
            waited = {}
            for o in per_eng[eng_name]:
                need = {}
                for d in o.deps:
                    p = ops[d]
                    if p.sem is None or ordered_free(p, o):
                        continue
                    if need.get(p.sem, 0) < p.ticket:
                        need[p.sem] = p.ticket
                for s, t in need.items():
                    if waited.get(s, 0) < t:
                        eng.wait_ge(sems[s], t)
                        waited[s] = t
                ins = o.fn(eng)
                if o.signal:
                    ins.then_inc(sems[o.sem], 16 if o.dma else 1)
            if eng_name == "sp":
                need = {}
                for p in final_waits:
                    if need.get(p.sem, 0) < p.ticket:
                        need[p.sem] = p.ticket
                for s, t in need.items():
                    if waited.get(s, 0) < t:
                        eng.wait_ge(sems[s], t)

        @block.sync
        def _(e):
            run("sp", e)

        @block.tensor
        def _(e):
            run("pe", e)

        @block.scalar
        def _(e):
            run("act", e)

        @block.vector
        def _(e):
            run("dve", e)

        @block.gpsimd
        def _(e):
            run("pool", e)


class Cfg:
    def __init__(self, D=4096, DC=2048, NH=16, DFF=11008, TT=512, NPT=4, NPRE=4, NSS=2, LS=32, NW=5):
        self.D, self.DC, self.NH, self.DFF = D, DC, NH, DFF
        self.TT, self.NPT, self.NPRE, self.NSS, self.LS, self.NW = TT, NPT, NPRE, NSS, LS, NW
        self.KC = D // 128
        self.NCG = DC // 128
        self.FC = DFF // 128
        self.DHG = NH * 128
        self.NIN = 3 * DC + 4 * self.DHG
        self.J = min(32, self.FC)
        self.KS = max(self.KC, self.J)
        self.TOFF = 0 if self.NCG >= 16 else self.KC
        self.NY = max(self.KC, self.J, self.TOFF + 16, (2 * D * 4) // (TT * 2))
        self.NHB = max(self.KC, (2 * D * 4) // (TT * 2))
        self.RA = 3 * self.KC
        self.RB = 3 * self.NCG + 2 * NH + 1
        assert self.RA <= 128 and self.RB <= 128
        assert D % 128 == 0 and DC % 128 == 0 and DFF % 128 == 0
        assert TT % 128 == 0 and NSS * LS <= TT and 128 % LS == 0


def build_program(cfg):
    C = cfg
    D, KC, NCG, NH, FC, TT, NW = C.D, C.KC, C.NCG, C.NH, C.FC, C.TT, C.NW
    nc = bass.Bass("TRN2", target_bir_lowering=False)

    def din(name, shape):
        return nc.dram_tensor(name, list(shape), F32, kind="ExternalInput").ap()

    def dout(name, shape):
        return nc.dram_tensor(name, list(shape), F32, kind="ExternalOutput").ap()

    xmain = din("xmain", [C.NPT * TT, D])
    xpre = din("xpre", [C.NPRE * TT, D])
    xsam = din("xsam", [C.NSS * C.LS, D])
    cconv = din("cconv", [C.NSS, 2 * NCG, 128])
    shg = din("shg", [C.NSS, NH, 128, 128])
    w_in = din("w_in", [D, C.NIN])
    w_out = din("w_out", [D, D])
    w_gate = din("w_gate", [D, C.DFF])
    w_up = din("w_up", [D, C.DFF])
    w_down = din("w_down", [C.DFF, D])
    vecA = din("vecA", [C.RA, 128])
    vecB = din("vecB", [C.RB, 128])
    cident = din("cident", [128, 128])
    cmask = din("cmask", [128, 128])
    y_main = dout("y_main", [C.NPT * TT, D])
    y_s = dout("y_s", [C.NSS * C.LS, D])
    conv_p = dout("conv_p", [2 * NCG, 128])
    hg_p = dout("hg_p", [NH, 128, 128])
    conv_s = dout("conv_s", [C.NSS, 2 * NCG, 128])
    hg_s = dout("hg_s", [C.NSS, NH, 128, 128])
    NCHUNK = 4 * NH + 3 * NCG + KC + 2 * FC + KC * (-(-FC // C.J))
    NWB = 4
    WSG = 96
    sspill = nc.dram_tensor("sspill", [NH, 128, 128], F32).ap()
    wscr_l = [nc.dram_tensor(f"wscr{g}", [min(WSG, NCHUNK - g * WSG), 128, C.KS * 128], BF16).ap() for g in range(-(-NCHUNK // WSG))]

    class _WS:
        def __getitem__(self, i):
            return wscr_l[i // WSG][i % WSG]
    wscr = _WS()

    es = ExitStack()
    with es:
        def sb(name, shape, dt):
            return es.enter_context(nc.sbuf_tensor(name, list(shape), dt))

        xT = sb("xT", [128, KC, TT], F32)
        hbuf = sb("hbuf", [128, C.NHB * TT // 2], F32)
        ybuf = sb("ybuf", [128, C.NY * TT // 2], F32)
        hT = hbuf.bitcast(BF16)[:].rearrange("p (c t) -> p c t", t=TT)
        yT = ybuf.bitcast(BF16)[:].rearrange("p (c t) -> p c t", t=TT)
        if NW >= 5:
            wslot = [sb(f"w{i}", [128, C.KS, 128], BF16) for i in range(3)]
            wa = sb("wa", [128, 2 * C.KS * 64], F32)
            wab = wa.bitcast(BF16)[:]
            wslot.append(wab[:, 0:C.KS * 128].rearrange("p (k c) -> p k c", c=128))
            wslot.append(wab[:, C.KS * 128:2 * C.KS * 128].rearrange("p (k c) -> p k c", c=128))
            wslot += [sb(f"w{i}", [128, C.KS, 128], BF16) for i in range(5, NW)]
        else:
            wslot = [sb(f"w{i}", [128, C.KS, 128], BF16) for i in range(NW)]
        NSEGMAX = 4
        S = sb("S", [128, C.NSS, NH, 128], F32)
        Sb = sb("Sb", [128, NSEGMAX, 128], BF16)
        convst = sb("convst", [128, C.NSS, 2, NCG], F32)
        tx = [sb(f"tx{i}", [128, TT], F32) for i in range(2)]
        bft = {n: sb("b_" + n, [128, TT], BF16) for n in ("qd", "kd", "qs", "kk")}
        vTb = [sb(f"vT{i}", [128, TT], BF16) for i in range(2)]
        sq = [sb(f"sq{i}", [128, TT], BF16) for i in range(2)]
        bft["osq"] = sq[0]
        attb = sb("attb", [128, NSEGMAX, 128], BF16)
        kktok = sb("kktok", [128, NSEGMAX, 128], BF16)
        vtok = sb("vtok", [128, NSEGMAX, 128], BF16)
        cub = sb("cub", [128, TT + 2 * C.NSS + 2], F32)
        nref = sb("nref", [128, NSEGMAX], F32)
        dec = [sb(f"dec{i}", [128, NSEGMAX], F32) for i in range(2)]
        ident = sb("ident", [128, 128], F32)
        identb = sb("identb", [128, 128], BF16)
        onesb = sb("onesb", [128, 128], BF16)
        onesf = sb("onesf", [128, 128], F32)
        maskf = sb("maskf", [128, 128], F32)
        maskr = sb("maskr", [128, NSEGMAX, 128], BF16)
        vstA = tx[0]
        vstB = tx[1]
        wn = sb("wn", [128, 3, KC], F32)
        vB = sb("vB", [128, C.RB], F32)
        lbv = sb("lbv", [128, 3, NH], F32)
        cst = sb("cst", [128, 128], F32)
        EPSC = sb("epsc", [128, 1], F32)
        cbak = sb("cbak", [128, 2, NCG], F32)
        TSR = C.NSS * C.LS
        x1s = wa[:, 0:C.KS * 64].rearrange("p (k n) -> p k n", n=64) if NW >= 5 else None
        h2s = wslot[4][:, :, 0:TSR] if NW >= 5 else None
        aTs = wslot[4][:, :, TSR:2 * TSR] if NW >= 5 else None
        ssq = sb("ssq", [128, 8], F32)
        rbc = sb("rbc", [128, 128], F32)
        hlast = sb("hlast", [128, KC, 2], BF16)
        pbank = [es.enter_context(nc.psum_tensor(f"pb{i}", [128, 512], F32)) for i in range(7)]
        ptr = es.enter_context(nc.psum_tensor("ptr", [128, 8, 128], BF16))

        sems = {}
        ndma = 12
        for e in ENGS:
            sems[e] = es.enter_context(nc.semaphore(f"s_{e}"))
        for e in ("sp", "pool"):
            for i in range(ndma):
                sems[("dma", e, i)] = es.enter_context(nc.semaphore(f"d_{e}_{i}"))
        block = es.enter_context(nc.Block())

        def tmp(i):
            return ybuf[:, (C.TOFF + 2 * i) * TT // 2:(C.TOFF + 2 * i + 2) * TT // 2]

        def tmpk(i):
            return [("y", C.TOFF + 2 * i), ("y", C.TOFF + 2 * i + 1)]

        def xin(k):
            return ybuf[:, k * D:(k + 1) * D]

        def xink(k):
            c0 = (k * D * 4) // (TT * 2)
            c1 = -(-((k + 1) * D * 4) // (TT * 2))
            return [("y", c) for c in range(c0, c1)]

        def ost(k):
            return hbuf[:, k * D:(k + 1) * D]

        def ostk(k):
            c0 = (k * D * 4) // (TT * 2)
            c1 = -(-((k + 1) * D * 4) // (TT * 2))
            return [("h", c) for c in range(c0, c1)]

        def pv4(i):
            return pbank[i][:].rearrange("p (a b) -> p a b", a=4)

        state = {}

        cidx_of = {}
        pre_cidx = set()
        slot_of = []
        prev_occ = []
        NMIX = 4 * NH + 3 * NCG + KC

        def emit_all(P, plan):
            dry = P.dry
            wst = {"i": 0, "issued": 0}
            rot = {"b": 0, "e": 0}

            def nextbank():
                b = rot["b"]
                rot["b"] = (b + 1) % 4
                return b

            def evac_eng():
                rot["e"] ^= 1
                return "act" if rot["e"] else "dve"

            def copy_op(eng, out, in_, reads, writes):
                if eng == "act":
                    P.op("act", lambda e: e.copy(out=out, in_=in_), reads, writes)
                else:
                    P.op("dve", lambda e: e.tensor_copy(out=out, in_=in_), reads, writes)

            mt_state = {"mt": None, "cidx": 0, "pt": None, "ns": NW, "nowb": False}

            def wbt(cidx):
                if C.NPT >= 4 and RIDE_ON and NW >= 5:
                    if cidx < NMIX:
                        return cidx % 3
                    r = cidx % NWB
                    return r if r < 2 else 99
                return cidx % NWB

            def use_scratch(ent):
                s_, n_, mt, cidx, pt, tag, ns, nowb = ent
                if mt is None:
                    return pt is not None and pt > 0 and tag in cidx_of
                return cidx in pre_cidx or mt > wbt(cidx)

            def issue_load(k):
                ent = plan[k]
                s_, n_, mt, cidx, pt, tag, ns, nowb = ent
                sl = slot_of[k]
                if use_scratch(ent):
                    ci = cidx if mt is not None else cidx_of[tag]
                    src = wscr[ci][:, :n_ * 128].rearrange("p (k c) -> p k c", c=128)
                    assert ("scr", ci) in P.last_w, ("scratch chunk read before write-back emitted", ci)
                    P.op("sp", (lambda e, sl=sl, src=src, n_=n_: e.dma_start(out=wslot[sl][:, :n_, :], in_=src)),
                         reads=[("scr", ci)], writes=[("w", sl)], dma=True)
                else:
                    P.op("pool", (lambda e, sl=sl, s_=s_, n_=n_: e.dma_start(out=wslot[sl][:, :n_, :], in_=s_)),
                         writes=[("w", sl)], dma=True)

            def prefetch(k):
                while wst["issued"] < len(plan):
                    j = wst["issued"]
                    po = prev_occ[j]
                    if j > k and not (po is None or po < k):
                        break
                    issue_load(j)
                    wst["issued"] += 1

            def wnext(W2d, r0, kcn, c0, tag=None):
                src = W2d[r0:r0 + kcn * 128, c0:c0 + 128].rearrange("(kc p) c -> p kc c", p=128)
                mt, cidx, pt = mt_state["mt"], mt_state["cidx"], mt_state["pt"]
                if mt is not None:
                    mt_state["cidx"] += 1
                if dry:
                    if mt == 0 and tag is not None and tag[0] in ("z", "i") and C.NPRE >= 1:
                        cidx_of[tag] = cidx
                        pre_cidx.add(cidx)
                    plan.append((src, kcn, mt, cidx, pt, tag, mt_state["ns"], mt_state["nowb"]))
                    return 0
                k = wst["i"]
                sl = slot_of[k]
                wst["i"] += 1
                while wst["issued"] <= k:
                    issue_load(wst["issued"])
                    wst["issued"] += 1
                wb = None
                if mt is None:
                    if pt == 0 and tag in cidx_of:
                        wb = cidx_of[tag]
                elif cidx not in pre_cidx and mt == wbt(cidx) and mt < C.NPT and not mt_state["nowb"]:
                    wb = cidx
                if wb is not None:
                    dst = wscr[wb][:, :kcn * 128].rearrange("p (k c) -> p k c", c=128)
                    P.op("sp", (lambda e, sl=sl, dst=dst, kcn=kcn: e.dma_start(out=dst, in_=wslot[sl][:, :kcn, :])),
                         reads=[("w", sl)], writes=[("scr", wb)], dma=True)
                prefetch(k)
                return sl

            def act(out, in_, func, reads, writes, bias=None, scale=None):
                kw = {}
                if bias is not None:
                    kw["bias"] = bias
                if scale is not None:
                    kw["scale"] = scale
                P.op("act", lambda e: e.activation(out=out, in_=in_, func=func, **kw), reads, writes)

            def tt(out, in0, in1, op, reads, writes, eng="dve"):
                P.op(eng, lambda e: e.tensor_tensor(out=out, in0=in0, in1=in1, op=op), reads, writes)

            def ts(out, in0, s1, s2, op0, op1, reads, writes, eng="dve"):
                if s2 is None:
                    P.op(eng, lambda e: e.tensor_scalar(out=out, in0=in0, scalar1=s1, scalar2=None, op0=op0), reads, writes)
                else:
                    P.op(eng, lambda e: e.tensor_scalar(out=out, in0=in0, scalar1=s1, scalar2=s2, op0=op0, op1=op1), reads, writes)

            def stt(out, in0, scalar, in1, op0, op1, reads, writes):
                P.op("dve", lambda e: e.scalar_tensor_tensor(out=out, in0=in0, scalar=scalar, in1=in1, op0=op0, op1=op1), reads, writes)

            def mm(out, lhsT, rhs, start, stop, reads, writes):
                P.op("pe", lambda e: e.matmul(out, lhsT, rhs, start=start, stop=stop), reads, writes)

            def tr(out, in_, idn, reads, writes):
                P.op("pe", lambda e: e.transpose(out, in_, idn), reads, writes)

            def mm_chunk(ps_ap, pskey, sl, kcn, rhs_fn, rkey_fn):
                for kc in range(kcn):
                    mm(ps_ap, wslot[sl][:, kc, :], rhs_fn(kc), kc == 0, kc == kcn - 1,
                       [("w", sl)] + rkey_fn(kc), [pskey])

            def setup():
                P.op("sp", lambda e: e.dma_start(out=ident[:], in_=cident[:, :]), writes=["ident"], dma=True)
                P.op("sp", lambda e: e.dma_start(out=maskf[:], in_=cmask[:, :]), writes=["maskf"], dma=True)
                P.op("sp", lambda e: e.dma_start(out=vstA[:C.RA, :128], in_=vecA[:, :]), writes=[("tx", 0)], dma=True)
                P.op("sp", lambda e: e.dma_start(out=vstB[:C.RB, :128], in_=vecB[:, :]), writes=[("tx", 1)], dma=True)
                copy_op("dve", identb[:], ident[:], ["ident"], ["identb"])
                P.op("dve", lambda e: e.memset(onesb[:], 1.0), writes=["onesb"])
                P.op("dve", lambda e: e.memset(onesf[:], 1.0), writes=["onesf"])
                for c in range(NSEGMAX):
                    copy_op("dve", maskr[:, c, :], maskf[:], ["maskf"], ["maskr"])
                P.op("dve", lambda e: e.memset(S[:].rearrange("p a b c -> p (a b c)"), 0.0),
                     writes=[("S", s_, h) for s_ in range(C.NSS) for h in range(NH)])
                P.op("dve", lambda e: e.memset(convst[:].rearrange("p a b c -> p (a b c)"), 0.0),
                     writes=[("cs", s_) for s_ in range(C.NSS)])
                tr(pbank[0][:, :C.RA], vstA[:C.RA, :128], ident[:C.RA, :C.RA], [("tx", 0), "ident"], [("pb", 0)])
                copy_op("dve", wn[:].rearrange("p a b -> p (a b)"), pbank[0][:, :C.RA], [("pb", 0)], ["wn"])
                tr(pbank[1][:, :C.RB], vstB[:C.RB, :128], ident[:C.RB, :C.RB], [("tx", 1), "ident"], [("pb", 1)])
                copy_op("dve", vB[:], pbank[1][:, :C.RB], [("pb", 1)], ["vB"])
                o0 = 3 * NCG
                tt(lbv[:, 1, :], vB[:, o0:o0 + NH], vB[:, o0 + NH:o0 + 2 * NH], ALU.subtract, ["vB"], ["lbv"])
                act(lbv[:, 0, :], lbv[:, 1, :], AF.Sigmoid, ["lbv"], ["lbv"])
                ts(lbv[:, 1, :], lbv[:, 0, :], -1.0, 1.0, ALU.mult, ALU.add, ["lbv"], ["lbv"])
                ts(lbv[:, 2, :], lbv[:, 0, :], -1.0, None, ALU.add, None, ["lbv"], ["lbv"])

            def cw(r, j):
                return vB[:, r * NCG + j:r * NCG + j + 1]

            hgw = vB[:, 3 * NCG + 2 * NH:3 * NCG + 2 * NH + 1]

            xpf = {"n": 0, "k0": 0}

            def prefetch_x(nxt, nmax=2, k0=0):
                if nxt is None:
                    return
                xsrc, T = nxt
                nsub = -(-T // 128)
                n = min(nmax, nsub)
                for sub in range(n):
                    rows = min(128, T - sub * 128)
                    kk = (sub + k0) % 2
                    st = xin(kk)
                    P.op("sp", (lambda e, st=st, sub=sub, rows=rows, xsrc=xsrc: e.dma_start(out=st[:rows, :], in_=xsrc[sub * 128:sub * 128 + rows, :])),
                         writes=xink(kk), dma=True)
                xpf["n"] = n
                xpf["k0"] = k0

            def load_tile(xsrc, T, widx):
                nsub = -(-T // 128)
                pss = pbank[4]
                for sub in range(nsub):
                    rows = min(128, T - sub * 128)
                    k = (sub + xpf["k0"]) % 2
                    st = xin(k)
                    cols = slice(sub * 128, sub * 128 + rows)
                    if sub >= xpf["n"]:
                        P.op("sp", (lambda e, st=st, sub=sub, rows=rows: e.dma_start(out=st[:rows, :], in_=xsrc[sub * 128:sub * 128 + rows, :])),
                             writes=xink(k), dma=True)
                    for c0 in range(0, KC, 4):
                        nb = min(4, KC - c0)
                        bi = nextbank()
                        bv = pv4(bi)
                        for i in range(nb):
                            c = c0 + i
                            tr(bv[:, i, :rows], st[:rows, c * 128:(c + 1) * 128], ident[:rows, :rows],
                               xink(k) + ["ident"], [("pb", bi)])
                        copy_op("act", xT[:, c0:c0 + nb, cols], bv[:, :nb, :rows],
                                [("pb", bi)], [("xT", c0 + i) for i in range(nb)] + [("xTs", c0 + i, sub) for i in range(nb)])
                    P.op("act", (lambda e, st=st, rows=rows, sub=sub: e.activation(out=st[:rows, :], in_=st[:rows, :], func=AF.Square,
                                                                                  accum_out=ssq[:rows, sub:sub + 1])),
                         xink(k), xink(k) + [("ssq", sub)])
                    act(ssq[:rows, 4 + sub:5 + sub], ssq[:rows, sub:sub + 1], AF.Ln, [("ssq", sub)], [("ssq2", sub)],
                        bias=EPSC[:rows, 0:1], scale=1.0 / D)
                    act(ssq[:rows, 4 + sub:5 + sub], ssq[:rows, 4 + sub:5 + sub], AF.Exp, [("ssq2", sub)], [("ssq2", sub)], scale=-0.5)
                    ts(rbc[:rows, :], onesf[:rows, :], ssq[:rows, 4 + sub:5 + sub], None, ALU.mult, None, [("ssq2", sub), "onesf"], ["rbc"])
                    tr(pss[:, cols], rbc[:rows, :], ident[:rows, :rows], ["rbc", "ident"], [("pb", 4)])
                    for c in range(KC):
                        stt(hT[:, c, cols], xT[:, c, cols], wn[:, widx, c:c + 1], pss[:, cols], ALU.mult, ALU.mult,
                            [("xTs", c, sub), "wn", ("pb", 4)], [("h", c)])

            def norm(T, widx, to_h):
                pss = pbank[4]
                for c in range(KC):
                    sqb = sq[c % 2]
                    act(sqb[:, :T], xT[:, c, :T], AF.Square, [("xT", c)], [("sq", c % 2)])
                    mm(pss[:, :T], onesb[:], sqb[:, :T], c == 0, c == KC - 1, [("sq", c % 2), "onesb"], [("pb", 4)])
                act(tx[0][:, :T], pss[:, :T], AF.Ln, [("pb", 4)], [("tx", 0)], bias=EPSC[:, 0:1], scale=1.0 / D)
                act(tx[1][:, :T], tx[0][:, :T], AF.Exp, [("tx", 0)], [("tx", 1)], scale=-0.5)
                for c in range(KC):
                    if to_h:
                        stt(hT[:, c, :T], xT[:, c, :T], wn[:, widx, c:c + 1], tx[1][:, :T], ALU.mult, ALU.mult,
                            [("xT", c), "wn", ("tx", 1)], [("h", c)])
                    else:
                        stt(xT[:, c, :T], xT[:, c, :T], wn[:, widx, c:c + 1], tx[1][:, :T], ALU.mult, ALU.mult,
                            [("xT", c), "wn", ("tx", 1)], [("xT", c)])

            def h_rhs(T, lo=0):
                return (lambda kc: hT[:, kc, lo:T]), (lambda kc: [("h", kc)])

            def make_head(h, T, segs, state_only):
                nseg = len(segs)
                L = segs[0][1]
                assert all(s_[1] == L and s_[0] == i * L for i, s_ in enumerate(segs)) and nseg * L == T
                ref = L // 2 - 1
                base = 3 * NCG
                rf, rk = h_rhs(T)
                cols = {"q": base + h, "z": base + NH + h, "i": base + 2 * NH + h, "o": base + 3 * NH + h}
                par = h % 2
                vT = vTb[par][:, :T]
                kvT = ("vT", par)
                t = [tmp(i)[:, :T] for i in range(8)]
                tk = [tmpk(i) for i in range(8)]
                lb_c, oml_c, noml_c = lbv[:, 0, h:h + 1], lbv[:, 1, h:h + 1], lbv[:, 2, h:h + 1]
                ps = {}
                attv, snv = pv4(4), pv4(6)
                po_ = pbank[5]

                def proj(n, kc0=0, kc1=None):
                    if kc0 == 0:
                        ps[n] = (wnext(w_in, 0, KC, cols[n] * 128, tag=(n, h)), nextbank())
                    sl, bi = ps[n]
                    kc1_ = KC if kc1 is None else kc1
                    for kc in range(kc0, kc1_):
                        mm(pbank[bi][:, :T], wslot[sl][:, kc, :], rf(kc), kc == 0, kc == KC - 1,
                           [("w", sl)] + rk(kc), [("pb", bi)])

                def A():
                    names = ("z", "i") if state_only else ("z", "i", "q")
                    for n in names:
                        proj(n)
                    pz, pi = pbank[ps["z"][1]][:, :T], pbank[ps["i"][1]][:, :T]
                    kz, ki = ("pb", ps["z"][1]), ("pb", ps["i"][1])
                    act(t[0], pz, AF.Sigmoid, [kz], tk[0])
                    if not state_only:
                        pq = pbank[ps["q"][1]][:, :T]
                        kq = ("pb", ps["q"][1])
                        act(t[1], pq, AF.Sigmoid, [kq], tk[1])
                    P.op("act", lambda e: e.copy(out=vT, in_=pi), [ki], [kvT])
                    if not state_only:
                        tt(t[3], pq, t[1], ALU.mult, [kq] + tk[1], tk[3])

                def B():
                    ts(t[5], t[0], noml_c, oml_c, ALU.mult, ALU.add, tk[0] + ["lbv"], tk[5])
                    act(t[6], t[0], AF.Ln, tk[0] + ["lbv"], tk[6], bias=lb_c, scale=oml_c)
                    for (c0, L_, sid) in segs:
                        P.op("dve", (lambda e, c0=c0: e.tensor_tensor_scan(out=t[7][:, c0:c0 + L], data0=onesf[:, :L], data1=t[6][:, c0:c0 + L],
                                                                          initial=0.0, op0=ALU.mult, op1=ALU.add)),
                             tk[6] + ["onesf"], tk[7])
                    lc3 = t[7].rearrange("p (s l) -> p s l", l=L)
                    if not state_only:
                        ts(nref[:, :nseg], lc3[:, :, ref], -1.0, None, ALU.mult, None, tk[7], ["nref"])
                    act(dec[par][:, :nseg], lc3[:, :, L - 1], AF.Exp, tk[7], [("dec", par)])
                    for ci, (c0, L_, sid) in enumerate(segs):
                        act(t[0][:, c0:c0 + L], t[7][:, c0:c0 + L], AF.Exp, tk[7], tk[0],
                            bias=t[7][:, c0 + L - 1:c0 + L], scale=-1.0)
                    tt(bft["kk"][:, :T], t[5], t[0], ALU.mult, tk[5] + tk[0], ["kk"])
                    if not state_only:
                        for ci, (c0, L_, sid) in enumerate(segs):
                            act(t[1][:, c0:c0 + L], t[7][:, c0:c0 + L], AF.Exp, tk[7] + ["nref"], tk[1],
                                bias=nref[:, ci:ci + 1], scale=1.0)
                        for ci, (c0, L_, sid) in enumerate(segs):
                            act(t[2][:, c0:c0 + L], t[7][:, c0:c0 + L], AF.Exp, tk[7], tk[2],
                                bias=t[7][:, c0 + ref:c0 + ref + 1], scale=-1.0)
                        act(t[6], t[7], AF.Exp, tk[7], tk[6])
                        tt(bft["qd"][:, :T], t[3], t[1], ALU.mult, tk[3] + tk[1], ["qd"])
                        tt(bft["kd"][:, :T], t[5], t[2], ALU.mult, tk[5] + tk[2], ["kd"])
                        tt(bft["qs"][:, :T], t[3], t[6], ALU.mult, tk[3] + tk[6], ["qs"])

                def Cst():
                    k1, k2 = KC // 3, (2 * KC) // 3
                    for ci, (c0, L_, sid) in enumerate(segs):
                        tr(ptr[:L, ci, :], bft["kk"][:, c0:c0 + L], identb[:], ["kk", "identb"], ["ptr"])
                        tr(ptr[:L, 4 + ci, :], vT[:, c0:c0 + L], identb[:], [kvT, "identb"], ["ptr"])
                        if not state_only:
                            mm(attv[:L, ci, :L], bft["kd"][:, c0:c0 + L], bft["qd"][:, c0:c0 + L], True, True,
                               ["kd", "qd"], [("pb", 4)])
                    P.op("act", lambda e: e.copy(out=kktok[:L, :nseg, :], in_=ptr[:L, 0:nseg, :]), ["ptr"], ["kktok"])
                    P.op("act", lambda e: e.copy(out=vtok[:L, :nseg, :], in_=ptr[:L, 4:4 + nseg, :]), ["ptr"], ["vtok"])
                    if not state_only:
                        tt(attb[:L, :nseg, :L], attv[:L, :nseg, :L], maskr[:L, :nseg, :L], ALU.mult,
                           [("pb", 4), "maskr"], ["attb"])
                        proj("o", 0, k1)
                    for ci, (c0, L_, sid) in enumerate(segs):
                        mm(snv[:, ci, :], kktok[:L, ci, :], vtok[:L, ci, :], True, True, ["kktok", "vtok"], [("pb", 6)])
                    for ci, (c0, L_, sid) in enumerate(segs):
                        if not state_only:
                            P.op("act", (lambda e, ci=ci, sid=sid: e.copy(out=Sb[:, ci, :], in_=S[:, sid, h, :])),
                                 [("S", sid, h)], [("Sb", ci)])
                        stt(S[:, sid, h, :], S[:, sid, h, :], dec[par][:, ci:ci + 1], snv[:, ci, :], ALU.mult, ALU.add,
                            [("S", sid, h), ("dec", par), ("pb", 6)], [("S", sid, h)])
                    if state_only:
                        return
                    proj("o", k1, k2)
                    for ci, (c0, L_, sid) in enumerate(segs):
                        mm(po_[:, c0:c0 + L], vtok[:L, ci, :], attb[:L, ci, :L], True, False, ["vtok", "attb"], [("pb", 5)])
                        mm(po_[:, c0:c0 + L], Sb[:, ci, :], bft["qs"][:, c0:c0 + L], False, True, [("Sb", ci), "qs"], [("pb", 5)])
                    proj("o", k2, KC)
                    pog = pbank[ps["o"][1]][:, :T]
                    kog = ("pb", ps["o"][1])
                    c0_, c1_ = tx[0][:, :T], tx[1][:, :T]
                    act(t[4], pog, AF.Sigmoid, [kog], tk[4])
                    tt(t[4], pog, t[4], ALU.mult, [kog] + tk[4], tk[4])
                    P.op("act", lambda e: e.copy(out=c0_, in_=po_[:, :T]), [("pb", 5)], [("tx", 0)])
                    act(bft["osq"][:, :T], po_[:, :T], AF.Square, [("pb", 5)], [("sq", 0)])
                    mm(pbank[4][:, :T], onesb[:], bft["osq"][:, :T], True, True, [("sq", 0), "onesb"], [("pb", 4)])
                    act(c1_, pbank[4][:, :T], AF.Ln, [("pb", 4)], [("tx", 1)], bias=EPSC[:, 0:1], scale=1.0 / 128)
                    act(c1_, c1_, AF.Exp, [("tx", 1)], [("tx", 1)], scale=-0.5)
                    stt(c0_, c0_, hgw, c1_, ALU.mult, ALU.mult, [("tx", 0), ("tx", 1), "vB"], [("tx", 0)])
                    tt(yT[:, NCG + h, :T], c0_, t[4], ALU.mult, [("tx", 0)] + tk[4], [("y", NCG + h)])

                return A, B, Cst

            def heads(T, segs, state_only):
                st = [make_head(h, T, segs, state_only) for h in range(NH)]
                st[0][0]()
                st[0][1]()
                for h in range(NH):
                    if h + 1 < NH:
                        st[h + 1][0]()
                    st[h][2]()
                    if h + 1 < NH:
                        st[h + 1][1]()

            def convgroup(j, T, cseqs, prefix_cu=False):
                rf, rk = h_rhs(T)
                ps = {}
                sls = {}
                for n, col in (("b", j), ("c", NCG + j), ("u", 2 * NCG + j)):
                    sl = wnext(w_in, 0, KC, col * 128)
                    bi = nextbank()
                    mm_chunk(pbank[bi][:, :T], ("pb", bi), sl, KC, rf, rk)
                    ps[n] = bi
                    if prefix_cu and n in ("c", "u"):
                        i_ = 0 if n == "c" else 1
                        for kc in range(KC):
                            mm(pbank[5][0:2, i_ * 128:(i_ + 1) * 128], hlast[:, kc, :], wslot[sl][:, kc, :], kc == 0, kc == KC - 1,
                               [("w", sl), "hlast"], [("pb", 5)])
                pb_, pc_, pu_ = (pbank[ps[n]] for n in ("b", "c", "u"))
                if prefix_cu:
                    p5 = pbank[5]
                    P.op("act", lambda e: e.copy(out=cst[0:2, :], in_=p5[0:2, 128:256]), [("pb", 5)], ["cst"])
                    tt(cst[0:2, :], p5[0:2, 0:128], cst[0:2, :], ALU.mult, [("pb", 5), "cst"], ["cst"])
                    tr(p5[:, 256:258], cst[0:2, :], ident[0:2, 0:2], ["cst", "ident"], [("pb", 5)])
                    P.op("act", lambda e: e.copy(out=convst[:, 0, :, j], in_=p5[:, 256:258]), [("pb", 5)], [("cs", 0)])
                P.op("act", lambda e: e.copy(out=tx[0][:, :T], in_=pu_[:, :T]), [("pb", ps["u"])], [("tx", 0)])
                for qi, (c0, n, cid) in enumerate(cseqs):
                    b0 = c0 + 2 * qi
                    P.op("act", (lambda e, b0=b0, cid=cid: e.copy(out=cub[:, b0:b0 + 2], in_=convst[:, cid, :, j])),
                         [("cs", cid)], ["cub"])
                    tt(cub[:, b0 + 2:b0 + 2 + n], pc_[:, c0:c0 + n], tx[0][:, c0:c0 + n], ALU.mult,
                       [("pb", ps["c"]), ("tx", 0)], ["cub"])
                    ts(tx[1][:, c0:c0 + n], cub[:, b0 + 2:b0 + 2 + n], cw(2, j), None, ALU.mult, None, ["cub", "vB"], [("tx", 1)])
                    stt(tx[1][:, c0:c0 + n], cub[:, b0 + 1:b0 + 1 + n], cw(1, j), tx[1][:, c0:c0 + n], ALU.mult, ALU.add,
                        ["cub", "vB", ("tx", 1)], [("tx", 1)])
                    stt(tx[1][:, c0:c0 + n], cub[:, b0:b0 + n], cw(0, j), tx[1][:, c0:c0 + n], ALU.mult, ALU.add,
                        ["cub", "vB", ("tx", 1)], [("tx", 1)])
                    tt(yT[:, j, c0:c0 + n], tx[1][:, c0:c0 + n], pb_[:, c0:c0 + n], ALU.mult,
                       [("tx", 1), ("pb", ps["b"])], [("y", j)])
                    P.op("act", (lambda e, b0=b0, n=n, cid=cid: e.copy(out=convst[:, cid, :, j], in_=cub[:, b0 + n:b0 + n + 2])),
                         ["cub"], [("cs", cid)])

            def conv_state_only(j, T, cid):
                rf = lambda kc: hT[:, kc, T - 2:T]
                rk = lambda kc: [("h", kc)]
                ps = {}
                for n, col in (("c", NCG + j), ("u", 2 * NCG + j)):
                    sl = wnext(w_in, 0, KC, col * 128)
                    bi = nextbank()
                    mm_chunk(pbank[bi][:, :2], ("pb", bi), sl, KC, rf, rk)
                    ps[n] = bi
                P.op("act", lambda e: e.copy(out=tx[0][:, :2], in_=pbank[ps["u"]][:, :2]), [("pb", ps["u"])], [("tx", 0)])
                tt(convst[:, cid, :, j], pbank[ps["c"]][:, :2], tx[0][:, :2], ALU.mult, [("pb", ps["c"]), ("tx", 0)], [("cs", cid)])

            def wout(T):
                for oc in range(KC):
                    sl = wnext(w_out, 0, KC, oc * 128)
                    bi = nextbank()
                    mm_chunk(pbank[bi][:, :T], ("pb", bi), sl, KC, (lambda kc: yT[:, kc, :T]), (lambda kc: [("y", kc)]))
                    tt(xT[:, oc, :T], pbank[bi][:, :T], xT[:, oc, :T], ALU.add, [("pb", bi), ("xT", oc)], [("xT", oc)])

            def ffn(T, ride=False):
                rf, rk = h_rhs(T)
                pG, pU, pD = pbank[4][:, 0:TSR], pbank[5][:, 0:TSR], pbank[6][:, 0:TSR]
                stx = cub[:, 0:2 * TSR].rearrange("p (k n) -> p k n", n=TSR)

                def smm(ps_ap, pskey, sl, kcn, rhs_fn, rkey_fn):
                    for kc in range(kcn):
                        mm(ps_ap, wslot[sl][:, kc, :], rhs_fn(kc), kc == 0, kc == kcn - 1,
                           [("w", sl)] + rkey_fn(kc), [pskey])

                for g0 in range(0, FC, C.J):
                    jn = min(C.J, FC - g0)
                    for jj in range(jn):
                        f = g0 + jj
                        k = jj % 2
                        sl = wnext(w_gate, 0, KC, f * 128)
                        bg = nextbank()
                        mm_chunk(pbank[bg][:, :T], ("pb", bg), sl, KC, rf, rk)
                        if ride:
                            smm(pG, ("pb", 4), sl, KC, (lambda kc: h2s[:, kc, :]), (lambda kc: [("hs", kc)]))
                        sl = wnext(w_up, 0, KC, f * 128)
                        bu = nextbank()
                        mm_chunk(pbank[bu][:, :T], ("pb", bu), sl, KC, rf, rk)
                        if ride:
                            smm(pU, ("pb", 5), sl, KC, (lambda kc: h2s[:, kc, :]), (lambda kc: [("hs", kc)]))
                        act(tx[k][:, :T], pbank[bg][:, :T], AF.Sigmoid, [("pb", bg)], [("tx", k)])
                        tt(tx[k][:, :T], pbank[bg][:, :T], tx[k][:, :T], ALU.mult, [("pb", bg), ("tx", k)], [("tx", k)])
                        tt(yT[:, jj, :T], tx[k][:, :T], pbank[bu][:, :T], ALU.mult, [("tx", k), ("pb", bu)], [("y", jj)])
                        if ride:
                            act(stx[:, k, :], pG, AF.Sigmoid, [("pb", 4)], [("stx", k)])
                            tt(stx[:, k, :], pG, stx[:, k, :], ALU.mult, [("pb", 4), ("stx", k)], [("stx", k)])
                            tt(aTs[:, jj, :], stx[:, k, :], pU, ALU.mult, [("stx", k), ("pb", 5)], [("as", jj)])
                    for oc in range(KC):
                        sl = wnext(w_down, g0 * 128, jn, oc * 128)
                        bi = nextbank()
                        mm_chunk(pbank[bi][:, :T], ("pb", bi), sl, jn, (lambda kc: yT[:, kc, :T]), (lambda kc: [("y", kc)]))
                        if ride:
                            smm(pD, ("pb", 6), sl, jn, (lambda kc: aTs[:, kc, :]), (lambda kc: [("as", kc)]))
                        tt(xT[:, oc, :T], pbank[bi][:, :T], xT[:, oc, :T], ALU.add, [("pb", bi), ("xT", oc)], [("xT", oc)])
                        if ride:
                            tt(x1s[:, oc, :], pD, x1s[:, oc, :], ALU.add, [("pb", 6), ("xs", oc)], [("xs", oc)])

            def store_y(ydst, T):
                norm(T, 2, False)
                nsub = -(-T // 128)
                for sub in range(nsub):
                    rows = min(128, T - sub * 128)
                    k = sub % 2
                    st = ost(k)
                    for c0 in range(0, KC, 4):
                        nb = min(4, KC - c0)
                        bi = nextbank()
                        bv = pv4(bi)
                        for i in range(nb):
                            c = c0 + i
                            tr(bv[:rows, i, :], xT[:, c, sub * 128:sub * 128 + rows], ident[:], [("xT", c), "ident"], [("pb", bi)])
                        copy_op(evac_eng(), st[:rows, c0 * 128:(c0 + nb) * 128].rearrange("p (a b) -> p a b", a=nb),
                                bv[:rows, :nb, :], [("pb", bi)], ostk(k))
                    P.op("sp", (lambda e, st=st, sub=sub, rows=rows: e.dma_start(out=ydst[sub * 128:sub * 128 + rows, :], in_=st[:rows, :])),
                         reads=ostk(k), dma=True, must_wait=True)

            def store_states(sid, hg_dst, conv_dst):
                P.op("sp", lambda e: e.dma_start(out=hg_dst.rearrange("h k v -> k h v"), in_=S[:, sid, :, :]),
                     reads=[("S", sid, h) for h in range(NH)], dma=True, must_wait=True)
                bi = nextbank()
                tr(pbank[bi][:2 * NCG, :128], convst[:, sid, :, :].rearrange("p a b -> p (a b)"), ident[:],
                   [("cs", sid), "ident"], [("pb", bi)])
                copy_op("dve", cst[:2 * NCG, :], pbank[bi][:2 * NCG, :128], [("pb", bi)], ["cst"])
                P.op("sp", lambda e: e.dma_start(out=conv_dst, in_=cst[:2 * NCG, :]), reads=["cst"], dma=True, must_wait=True)

            def load_states(sid, hg_src, conv_src):
                P.op("sp", lambda e: e.dma_start(out=S[:, sid, :, :], in_=hg_src.rearrange("h k v -> k h v")),
                     writes=[("S", sid, h) for h in range(NH)], dma=True)
                P.op("sp", lambda e: e.dma_start(out=cst[:2 * NCG, :], in_=conv_src), writes=["cst"], dma=True)
                bi = nextbank()
                tr(pbank[bi][:, :2 * NCG], cst[:2 * NCG, :], ident[:2 * NCG, :2 * NCG], ["cst", "ident"], [("pb", bi)])
                copy_op("dve", convst[:, sid, :, :].rearrange("p a b -> p (a b)"), pbank[bi][:, :2 * NCG], [("pb", bi)], [("cs", sid)])

            def main_tile(mt, xsrc, ydst, T, segs, cseqs, ride=False, nxt=None):
                mt_state["mt"] = mt
                mt_state["pt"] = None
                mt_state["cidx"] = 0
                mt_state["nowb"] = False
                load_tile(xsrc, T, 0)
                xpf["n"] = 0
                xpf["k0"] = 0
                heads(T, segs, False)
                for j in range(NCG):
                    convgroup(j, T, cseqs, prefix_cu=(mt == 0))
                wout(T)
                norm(T, 1, True)
                ffn(T, ride)
                prefetch_x(nxt)
                store_y(ydst, T)

            def sample_front(mt, xsrc, T, segs, cseqs, nxt=None):
                mt_state["mt"] = mt
                mt_state["pt"] = None
                mt_state["cidx"] = 0
                mt_state["nowb"] = True
                load_tile(xsrc, T, 0)
                xpf["n"] = 0
                xpf["k0"] = 0
                heads(T, segs, False)
                for j in range(NCG):
                    convgroup(j, T, cseqs)
                wout(T)
                norm(T, 1, True)
                mt_state["nowb"] = False
                mt_state["ns"] = 3
                P.op("dve", lambda e: e.tensor_copy(out=x1s[:, :KC, :], in_=xT[:, :, :T]),
                     [("xT", c) for c in range(KC)], [("xs", c) for c in range(KC)] + [("w", 3)])
                P.op("dve", lambda e: e.tensor_copy(out=h2s[:, :KC, :], in_=hT[:, :, :T]),
                     [("h", c) for c in range(KC)], [("hs", c) for c in range(KC)] + [("as", c) for c in range(C.KS)] + [("w", 4)])
                prefetch_x(nxt)

            def sample_back(ydst, T):
                P.op("dve", lambda e: e.tensor_copy(out=xT[:, :, :T], in_=x1s[:, :KC, :]),
                     [("xs", c) for c in range(KC)], [("xT", c) for c in range(KC)])
                store_y(ydst, T)

            def prefix_tile(pt, xsrc, T, segs, last, nxt=None):
                mt_state["mt"] = None
                mt_state["pt"] = pt
                load_tile(xsrc, T, 0)
                xpf["n"] = 0
                xpf["k0"] = 0
                heads(T, segs, True)
                if C.TOFF == 0 and 16 * TT * 2 >= D * 4:
                    prefetch_x(nxt, nmax=1, k0=1)
                if last:
                    copy_op("dve", hlast[:], hT[:, :, T - 2:T], [("h", c) for c in range(KC)], ["hlast"])

            setup()
            pseg = [(i * 128, 128, 0) for i in range(TT // 128)]
            for t_ in range(C.NPRE):
                nx_ = (xpre[(t_ + 1) * TT:(t_ + 2) * TT, :], TT) if t_ + 1 < C.NPRE else (xmain[0:TT, :], TT)
                prefix_tile(t_, xpre[t_ * TT:(t_ + 1) * TT, :], TT, pseg, t_ == C.NPRE - 1, nxt=nx_)
            TS = C.NSS * C.LS
            sseg = [(i * C.LS, C.LS, i) for i in range(C.NSS)]
            RIDE = (NW >= 5 and C.NPT >= 2 and TS == TSR) and RIDE_ON
            for t_ in range(C.NPT):
                last = (t_ == C.NPT - 1)
                if last and RIDE:
                    P.op("sp", lambda e: e.dma_start(out=sspill.rearrange("h k v -> k h v"), in_=S[:, 0, :, :]),
                         reads=[("S", 0, h) for h in range(NH)], writes=["sspill"], dma=True)
                    copy_op("dve", cbak[:], convst[:, 0, :, :], [("cs", 0)], ["cbak"])
                    for s_ in range(C.NSS):
                        load_states(s_, shg[s_], cconv[s_])
                    sample_front(t_, xsam, TS, sseg, sseg, nxt=(xmain[t_ * TT:(t_ + 1) * TT, :], TT))
                    for s_ in range(C.NSS):
                        store_states(s_, hg_s[s_], conv_s[s_])
                    P.op("sp", lambda e: e.dma_start(out=S[:, 0, :, :], in_=sspill.rearrange("h k v -> k h v")),
                         reads=["sspill"], writes=[("S", 0, h) for h in range(NH)], dma=True)
                    copy_op("dve", convst[:, 0, :, :], cbak[:], ["cbak"], [("cs", 0)])
                if t_ + 1 < C.NPT:
                    nxt_ = (xsam, TS) if (t_ + 1 == C.NPT - 1 and RIDE) else (xmain[(t_ + 1) * TT:(t_ + 2) * TT, :], TT)
                else:
                    nxt_ = None if RIDE else (xsam, TS)
                main_tile(t_, xmain[t_ * TT:(t_ + 1) * TT, :], y_main[t_ * TT:(t_ + 1) * TT, :], TT, pseg, [(0, TT, 0)],
                          ride=(last and RIDE and RIDE_FFN), nxt=nxt_)
            store_states(0, hg_p, conv_p[:, :])
            if RIDE:
                sample_back(y_s, TS)
            else:
                for s_ in range(C.NSS):
                    load_states(s_, shg[s_], cconv[s_])
                main_tile(C.NPT, xsam, y_s, TS, sseg, sseg)
                for s_ in range(C.NSS):
                    store_states(s_, hg_s[s_], conv_s[s_])

        Pd = Prog(dry=True)
        plan = []
        emit_all(Pd, plan)
        rr = 0
        last_in = {}
        for k_, ent_ in enumerate(plan):
            sl_ = rr % ent_[6]
            rr += 1
            slot_of.append(sl_)
            prev_occ.append(last_in.get(sl_))
            last_in[sl_] = k_
        P = Prog(dry=False, n_dma_sems=ndma)
        P.op("dve", lambda e: e.memset(EPSC[:], EPS), writes=["epsc"])
        emit_all(P, plan)
        state["nops"] = len(P.ops)
        P.emit(block, sems)
    return nc, state


_CACHE = {}


def _consts():
    ident = np.eye(128, dtype=np.float32)
    s = np.arange(128)[:, None]
    t = np.arange(128)[None, :]
    mask = (t >= s).astype(np.float32)
    return ident, mask


def run_cfg(cfg, inputs, n_cores=8, trace=False):
    C = cfg
    key = (C.D, C.DC, C.NH, C.DFF, C.TT, C.NPT, C.NPRE, C.NSS, C.LS, C.NW)
    if key not in _CACHE:
        _CACHE[key] = build_program(C)
    nc, st = _CACHE[key]
    f = lambda a: np.ascontiguousarray(np.asarray(a, dtype=np.float32))
    xp, xs = f(inputs["x_prompt"]), f(inputs["x_sample"])
    B, SEQ, D = xp.shape
    half = SEQ // 2
    assert half == C.NPT * C.TT == C.NPRE * C.TT and n_cores == 2 * B
    assert xs.shape[0] == C.NSS * n_cores and xs.shape[1] == C.LS
    cache_conv, state_hgrn = f(inputs["cache_conv"]), f(inputs["state_hgrn"])
    w_in, w_out = f(inputs["w_in"][0]), f(inputs["w_out"][0])
    w_gate, w_up, w_down = f(inputs["w_gate"][0]), f(inputs["w_up"][0]), f(inputs["w_down"][0])
    vecA = np.concatenate([f(inputs["norm_mix"][0]).reshape(C.KC, 128), f(inputs["norm_ffn"][0]).reshape(C.KC, 128),
                           f(inputs["norm_final"]).reshape(C.KC, 128)], axis=0)
    vecB = np.concatenate([f(inputs["conv_w"][0]).reshape(3 * C.NCG, 128), f(inputs["lb_logits"]).reshape(2 * C.NH, 128),
                           f(inputs["hg_norm"][0]).reshape(1, 128)], axis=0)
    ident, mask = _consts()
    zeros_pre = np.zeros((half, D), np.float32)
    in_maps = []
    for c in range(n_cores):
        s, hf = c // 2, c % 2
        in_maps.append({
            "xmain": np.ascontiguousarray(xp[s, hf * half:(hf + 1) * half]),
            "xpre": np.ascontiguousarray(xp[s, :half]) if hf == 1 else zeros_pre,
            "xsam": np.ascontiguousarray(xs[C.NSS * c:C.NSS * (c + 1)].reshape(C.NSS * C.LS, D)),
            "cconv": np.ascontiguousarray(cache_conv[0, C.NSS * c:C.NSS * (c + 1)].reshape(C.NSS, 2 * C.NCG, 128)),
            "shg": np.ascontiguousarray(state_hgrn[0, C.NSS * c:C.NSS * (c + 1)]),
            "w_in": w_in, "w_out": w_out, "w_gate": w_gate, "w_up": w_up, "w_down": w_down,
            "vecA": vecA, "vecB": vecB, "cident": ident, "cmask": mask,
        })
    res = run_bass_kernel_spmd(nc, in_maps, core_ids=list(range(n_cores)), trace=trace)
    R = res.results
    y_prompt = np.empty((B, SEQ, D), np.float32)
    y_sample = np.empty(xs.shape, np.float32)
    ncp = np.empty((1, B, 2, C.DC), np.float32)
    nhp = np.empty((1, B, C.NH, 128, 128), np.float32)
    ncs = np.empty((1, xs.shape[0], 2, C.DC), np.float32)
    nhs = np.empty((1, xs.shape[0], C.NH, 128, 128), np.float32)
    for c in range(n_cores):
        s, hf = c // 2, c % 2
        r = R[c]
        y_prompt[s, hf * half:(hf + 1) * half] = r["y_main"]
        y_sample[C.NSS * c:C.NSS * (c + 1)] = r["y_s"].reshape(C.NSS, C.LS, D)
        if hf == 1:
            ncp[0, s] = r["conv_p"].reshape(2, C.DC)
            nhp[0, s] = r["hg_p"]
        ncs[0, C.NSS * c:C.NSS * (c + 1)] = r["conv_s"].reshape(C.NSS, 2, C.DC)
        nhs[0, C.NSS * c:C.NSS * (c + 1)] = r["hg_s"]
    out = (y_prompt, y_sample, ncp, nhp, ncs, nhs)
    if trace:
        return out, res
    return out


def kernel(**inputs):
    return run_cfg(Cfg(), inputs)
```

```python
import math
import numpy as np
from contextlib import ExitStack
import concourse.bass as bass
import concourse.mybir as mybir
from concourse.bass_utils import run_bass_kernel_spmd

F32 = mybir.dt.float32
BF16 = mybir.dt.bfloat16
AF = mybir.ActivationFunctionType
ALU = mybir.AluOpType
EPS = 1e-6
RIDE_ON = True
RIDE_FFN = True
RIDE_DOWN = True
ENGS = ("pe", "act", "dve", "pool", "sp")


class Op:
    __slots__ = ("eng", "fn", "deps", "dma", "signal", "ticket", "sem", "idx")

    def __init__(self, eng, fn, dma):
        self.eng = eng
        self.fn = fn
        self.dma = dma
        self.deps = set()
        self.signal = False
        self.ticket = None
        self.sem = None
        self.idx = None


class Prog:
    def __init__(self, dry=False, n_dma_sems=12):
        self.dry = dry
        self.ops = []
        self.last_w = {}
        self.readers = {}
        self.n_dma_sems = n_dma_sems
        self.final_waits = []

    def op(self, eng, fn, reads=(), writes=(), dma=False, must_wait=False):
        if self.dry:
            return None
        o = Op(eng, fn, dma)
        o.idx = len(self.ops)
        deps = set()
        for r in reads:
            w = self.last_w.get(r)
            if w is not None:
                deps.add(w)
        for w_ in writes:
            w = self.last_w.get(w_)
            if w is not None:
                deps.add(w)
            rd = self.readers.get(w_)
            if rd:
                deps.update(rd[0].values())
                deps.update(rd[1])
        o.deps = deps
        for r in reads:
            rd = self.readers.get(r)
            if rd is None:
                rd = self.readers[r] = ({}, [])
            if dma:
                rd[1].append(o.idx)
            else:
                rd[0][eng] = o.idx
        for w_ in writes:
            self.last_w[w_] = o.idx
            self.readers[w_] = ({}, [])
        self.ops.append(o)
        if must_wait:
            self.final_waits.append(o.idx)
        return o

    def emit(self, block, sems):
        ops = self.ops

        def ordered_free(p, o):
            return p.eng == "pe" and o.eng == "pe" and not p.dma and not o.dma

        for o in ops:
            for d in o.deps:
                p = ops[d]
                if ordered_free(p, o):
                    continue
                p.signal = True
        for i in self.final_waits:
            ops[i].signal = True
        cnt = {e: 0 for e in ENGS}
        dma_rr = {e: 0 for e in ENGS}
        dma_cnt = {}
        dma_prev = {}
        for o in ops:
            if o.dma:
                si = dma_rr[o.eng] % self.n_dma_sems
                dma_rr[o.eng] += 1
                key = (o.eng, si)
                prev = dma_prev.get(key)
                if prev is not None:
                    o.deps.add(prev)
                dma_prev[key] = o.idx
                dma_cnt[key] = dma_cnt.get(key, 0) + 16
                o.sem = ("dma",) + key
                o.ticket = dma_cnt[key]
                o.signal = True
            elif o.signal:
                cnt[o.eng] += 1
                o.sem = o.eng
                o.ticket = cnt[o.eng]
        per_eng = {e: [o for o in ops if o.eng == e] for e in ENGS}
        final_waits = [ops[i] for i in self.final_waits]

        def run(eng_name, eng):
            waited = {}
            for o in per_eng[eng_name]:
                need = {}
                for d in o.deps:
                    p = ops[d]
                    if p.sem is None or ordered_free(p, o):
                        continue
                    if need.get(p.sem, 0) < p.ticket:
                        need[p.sem] = p.ticket
                for s, t in need.items():
                    if waited.get(s, 0) < t:
                        eng.wait_ge(sems[s], t)
                        waited[s] = t
                ins = o.fn(eng)
                if o.signal:
                    ins.then_inc(sems[o.sem], 16 if o.dma else 1)
            if eng_name == "sp":
                need = {}
                for p in final_waits:
                    if need.get(p.sem, 0) < p.ticket:
                        need[p.sem] = p.ticket
                for s, t in need.items():
                    if waited.get(s, 0) < t:
                        eng.wait_ge(sems[s], t)

        @block.sync
        def _(e):
            run("sp", e)

        @block.tensor
        def _(e):
            run("pe", e)

        @block.scalar
        def _(e):
            run("act", e)

        @block.vector
        def _(e):
            run("dve", e)

        @block.gpsimd
        def _(e):
            run("pool", e)


class Cfg:
    def __init__(self, D=4096, DC=2048, NH=16, DFF=11008, TT=512, NPT=4, NPRE=4, NSS=2, LS=32, NW=5):
        self.D, self.DC, self.NH, self.DFF = D, DC, NH, DFF
        self.TT, self.NPT, self.NPRE, self.NSS, self.LS, self.NW = TT, NPT, NPRE, NSS, LS, NW
        self.KC = D // 128
        self.NCG = DC // 128
        self.FC = DFF // 128
        self.DHG = NH * 128
        self.NIN = 3 * DC + 4 * self.DHG
        self.J = min(32, self.FC)
        self.KS = max(self.KC, self.J)
        self.TOFF = 0 if self.NCG >= 16 else self.KC
        self.NY = max(self.KC, self.J, self.TOFF + 16, (2 * D * 4) // (TT * 2))
        self.NHB = max(self.KC, (2 * D * 4) // (TT * 2))
        self.RA = 3 * self.KC
        self.RB = 3 * self.NCG + 2 * NH + 1
        assert self.RA <= 128 and self.RB <= 128
        assert D % 128 == 0 and DC % 128 == 0 and DFF % 128 == 0
        assert TT % 128 == 0 and NSS * LS <= TT and 128 % LS == 0


def build_program(cfg):
    C = cfg
    D, KC, NCG, NH, FC, TT, NW = C.D, C.KC, C.NCG, C.NH, C.FC, C.TT, C.NW
    nc = bass.Bass("TRN2", target_bir_lowering=False)

    def din(name, shape):
        return nc.dram_tensor(name, list(shape), F32, kind="ExternalInput").ap()

    def dout(name, shape):
        return nc.dram_tensor(name, list(shape), F32, kind="ExternalOutput").ap()

    xmain = din("xmain", [C.NPT * TT, D])
    xpre = din("xpre", [C.NPRE * TT, D])
    xsam = din("xsam", [C.NSS * C.LS, D])
    cconv = din("cconv", [C.NSS, 2 * NCG, 128])
    shg = din("shg", [C.NSS, NH, 128, 128])
    w_in = din("w_in", [D, C.NIN])
    w_out = din("w_out", [D, D])
    w_gate = din("w_gate", [D, C.DFF])
    w_up = din("w_up", [D, C.DFF])
    w_down = din("w_down", [C.DFF, D])
    vecA = din("vecA", [C.RA, 128])
    vecB = din("vecB", [C.RB, 128])
    cident = din("cident", [128, 128])
    cmask = din("cmask", [128, 128])
    y_main = dout("y_main", [C.NPT * TT, D])
    y_s = dout("y_s", [C.NSS * C.LS, D])
    conv_p = dout("conv_p", [2 * NCG, 128])
    hg_p = dout("hg_p", [NH, 128, 128])
    conv_s = dout("conv_s", [C.NSS, 2 * NCG, 128])
    hg_s = dout("hg_s", [C.NSS, NH, 128, 128])
    NCHUNK = 4 * NH + 3 * NCG + KC + 2 * FC + KC * (-(-FC // C.J))
    NWB = 4
    WSG = 96
    sspill = nc.dram_tensor("sspill", [NH, 128, 128], F32).ap()
    wscr_l = [nc.dram_tensor(f"wscr{g}", [min(WSG, NCHUNK - g * WSG), 128, C.KS * 128], BF16).ap() for g in range(-(-NCHUNK // WSG))]

    class _WS:
        def __getitem__(self, i):
            return wscr_l[i // WSG][i % WSG]
    wscr = _WS()

    es = ExitStack()
    with es:
        def sb(name, shape, dt):
            return es.enter_context(nc.sbuf_tensor(name, list(shape), dt))

        xT = sb("xT", [128, KC, TT], F32)
        hbuf = sb("hbuf", [128, C.NHB * TT // 2], F32)
        ybuf = sb("ybuf", [128, C.NY * TT // 2], F32)
        hT = hbuf.bitcast(BF16)[:].rearrange("p (c t) -> p c t", t=TT)
        yT = ybuf.bitcast(BF16)[:].rearrange("p (c t) -> p c t", t=TT)
        if NW >= 5:
            wslot = [sb(f"w{i}", [128, C.KS, 128], BF16) for i in range(3)]
            wa = sb("wa", [128, 2 * C.KS * 64], F32)
            wab = wa.bitcast(BF16)[:]
            wslot.append(wab[:, 0:C.KS * 128].rearrange("p (k c) -> p k c", c=128))
            wslot.append(wab[:, C.KS * 128:2 * C.KS * 128].rearrange("p (k c) -> p k c", c=128))
            wslot += [sb(f"w{i}", [128, C.KS, 128], BF16) for i in range(5, NW)]
        else:
            wslot = [sb(f"w{i}", [128, C.KS, 128], BF16) for i in range(NW)]
        NSEGMAX = 4
        S = sb("S", [128, C.NSS, NH, 128], F32)
        Sb = sb("Sb", [128, NSEGMAX, 128], BF16)
        convst = sb("convst", [128, C.NSS, 2, NCG], F32)
        tx = [sb(f"tx{i}", [128, TT], F32) for i in range(2)]
        bft = {n: sb("b_" + n, [128, TT], BF16) for n in ("qd", "kd", "qs", "kk")}
        vTb = [sb(f"vT{i}", [128, TT], BF16) for i in range(2)]
        sq = [sb(f"sq{i}", [128, TT], BF16) for i in range(2)]
        bft["osq"] = sq[0]
        attb = sb("attb", [128, NSEGMAX, 128], BF16)
        kktok = sb("kktok", [128, NSEGMAX, 128], BF16)
        vtok = sb("vtok", [128, NSEGMAX, 128], BF16)
        cub = sb("cub", [128, TT + 2 * C.NSS + 2], F32)
        nref = sb("nref", [128, NSEGMAX], F32)
        dec = [sb(f"dec{i}", [128, NSEGMAX], F32) for i in range(2)]
        ident = sb("ident", [128, 128], F32)
        identb = sb("identb", [128, 128], BF16)
        onesb = sb("onesb", [128, 128], BF16)
        onesf = sb("onesf", [128, 128], F32)
        maskf = sb("maskf", [128, 128], F32)
        maskr = sb("maskr", [128, NSEGMAX, 128], BF16)
        vstA = tx[0]
        vstB = tx[1]
        wn = sb("wn", [128, 3, KC], F32)
        vB = sb("vB", [128, C.RB], F32)
        lbv = sb("lbv", [128, 3, NH], F32)
        cst = sb("cst", [128, 128], F32)
        EPSC = sb("epsc", [128, 1], F32)
        cbak = sb("cbak", [128, 2, NCG], F32)
        TSR = C.NSS * C.LS
        x1s = wa[:, 0:C.KS * 64].rearrange("p (k n) -> p k n", n=64) if NW >= 5 else None
        h2s = wslot[4][:, :, 0:TSR] if NW >= 5 else None
        aTs = wslot[4][:, :, TSR:2 * TSR] if NW >= 5 else None
        ssq = sb("ssq", [128, 8], F32)
        rbc = sb("rbc", [128, 128], F32)
        hlast = sb("hlast", [128, KC, 2], BF16)
        pbank = [es.enter_context(nc.psum_tensor(f"pb{i}", [128, 512], F32)) for i in range(7)]
        ptr = es.enter_context(nc.psum_tensor("ptr", [128, 8, 128], BF16))

        sems = {}
        ndma = 12
        for e in ENGS:
            sems[e] = es.enter_context(nc.semaphore(f"s_{e}"))
        for e in ("sp", "pool"):
            for i in range(ndma):
                sems[("dma", e, i)] = es.enter_context(nc.semaphore(f"d_{e}_{i}"))
        block = es.enter_context(nc.Block())

        def tmp(i):
            return ybuf[:, (C.TOFF + 2 * i) * TT // 2:(C.TOFF + 2 * i + 2) * TT // 2]

        def tmpk(i):
            return [("y", C.TOFF + 2 * i), ("y", C.TOFF + 2 * i + 1)]

        def xin(k):
            return ybuf[:, k * D:(k + 1) * D]

        def xink(k):
            c0 = (k * D * 4) // (TT * 2)
            c1 = -(-((k + 1) * D * 4) // (TT * 2))
            return [("y", c) for c in range(c0, c1)]

        def ost(k):
            return hbuf[:, k * D:(k + 1) * D]

        def ostk(k):
            c0 = (k * D * 4) // (TT * 2)
            c1 = -(-((k + 1) * D * 4) // (TT * 2))
            return [("h", c) for c in range(c0, c1)]

        def pv4(i):
            return pbank[i][:].rearrange("p (a b) -> p a b", a=4)

        state = {}

        cidx_of = {}
        pre_cidx = set()
        slot_of = []
        prev_occ = []
        NMIX = 4 * NH + 3 * NCG + KC

        def emit_all(P, plan):
            dry = P.dry
            wst = {"i": 0, "issued": 0}
            rot = {"b": 0, "e": 0}

            def nextbank():
                b = rot["b"]
                rot["b"] = (b + 1) % 4
                return b

            def evac_eng():
                rot["e"] ^= 1
                return "act" if rot["e"] else "dve"

            def copy_op(eng, out, in_, reads, writes):
                if eng == "act":
                    P.op("act", lambda e: e.copy(out=out, in_=in_), reads, writes)
                else:
                    P.op("dve", lambda e: e.tensor_copy(out=out, in_=in_), reads, writes)

            mt_state = {"mt": None, "cidx": 0, "pt": None, "ns": NW, "nowb": False}

            def wbt(cidx):
                if C.NPT >= 4 and RIDE_ON and NW >= 5:
                    if cidx < NMIX:
                        return cidx % 3
                    r = cidx % NWB
                    return r if r < 2 else 99
                return cidx % NWB

            def use_scratch(ent):
                s_, n_, mt, cidx, pt, tag, ns, nowb = ent
                if mt is None:
                    return pt is not None and pt > 0 and tag in cidx_of
                return cidx in pre_cidx or mt > wbt(cidx)

            def issue_load(k):
                ent = plan[k]
                s_, n_, mt, cidx, pt, tag, ns, nowb = ent
                sl = slot_of[k]
                if use_scratch(ent):
                    ci = cidx if mt is not None else cidx_of[tag]
                    src = wscr[ci][:, :n_ * 128].rearrange("p (k c) -> p k c", c=128)
                    assert ("scr", ci) in P.last_w, ("scratch chunk read before write-back emitted", ci)
                    P.op("sp", (lambda e, sl=sl, src=src, n_=n_: e.dma_start(out=wslot[sl][:, :n_, :], in_=src)),
                         reads=[("scr", ci)], writes=[("w", sl)], dma=True)
                else:
                    P.op("pool", (lambda e, sl=sl, s_=s_, n_=n_: e.dma_start(out=wslot[sl][:, :n_, :], in_=s_)),
                         writes=[("w", sl)], dma=True)

            def prefetch(k):
                while wst["issued"] < len(plan):
                    j = wst["issued"]
                    po = prev_occ[j]
                    if j > k and not (po is None or po < k):
                        break
                    issue_load(j)
                    wst["issued"] += 1

            def wnext(W2d, r0, kcn, c0, tag=None):
                src = W2d[r0:r0 + kcn * 128, c0:c0 + 128].rearrange("(kc p) c -> p kc c", p=128)
                mt, cidx, pt = mt_state["mt"], mt_state["cidx"], mt_state["pt"]
                if mt is not None:
                    mt_state["cidx"] += 1
                if dry:
                    if mt == 0 and tag is not None and tag[0] in ("z", "i") and C.NPRE >= 1:
                        cidx_of[tag] = cidx
                        pre_cidx.add(cidx)
                    plan.append((src, kcn, mt, cidx, pt, tag, mt_state["ns"], mt_state["nowb"]))
                    return 0
                k = wst["i"]
                sl = slot_of[k]
                wst["i"] += 1
                while wst["issued"] <= k:
                    issue_load(wst["issued"])
                    wst["issued"] += 1
                wb = None
                if mt is None:
                    if pt == 0 and tag in cidx_of:
                        wb = cidx_of[tag]
                elif cidx not in pre_cidx and mt == wbt(cidx) and mt < C.NPT and not mt_state["nowb"]:
                    wb = cidx
                if wb is not None:
                    dst = wscr[wb][:, :kcn * 128].rearrange("p (k c) -> p k c", c=128)
                    P.op("sp", (lambda e, sl=sl, dst=dst, kcn=kcn: e.dma_start(out=dst, in_=wslot[sl][:, :kcn, :])),
                         reads=[("w", sl)], writes=[("scr", wb)], dma=True)
                prefetch(k)
                return sl

            def act(out, in_, func, reads, writes, bias=None, scale=None):
                kw = {}
                if bias is not None:
                    kw["bias"] = bias
                if scale is not None:
                    kw["scale"] = scale
                P.op("act", lambda e: e.activation(out=out, in_=in_, func=func, **kw), reads, writes)

            def tt(out, in0, in1, op, reads, writes, eng="dve"):
                P.op(eng, lambda e: e.tensor_tensor(out=out, in0=in0, in1=in1, op=op), reads, writes)

            def ts(out, in0, s1, s2, op0, op1, reads, writes, eng="dve"):
                if s2 is None:
                    P.op(eng, lambda e: e.tensor_scalar(out=out, in0=in0, scalar1=s1, scalar2=None, op0=op0), reads, writes)
                else:
                    P.op(eng, lambda e: e.tensor_scalar(out=out, in0=in0, scalar1=s1, scalar2=s2, op0=op0, op1=op1), reads, writes)

            def stt(out, in0, scalar, in1, op0, op1, reads, writes):
                P.op("dve", lambda e: e.scalar_tensor_tensor(out=out, in0=in0, scalar=scalar, in1=in1, op0=op0, op1=op1), reads, writes)

            def mm(out, lhsT, rhs, start, stop, reads, writes):
                P.op("pe", lambda e: e.matmul(out, lhsT, rhs, start=start, stop=stop), reads, writes)

            def tr(out, in_, idn, reads, writes):
                P.op("pe", lambda e: e.transpose(out, in_, idn), reads, writes)

            def mm_chunk(ps_ap, pskey, sl, kcn, rhs_fn, rkey_fn):
                for kc in range(kcn):
                    mm(ps_ap, wslot[sl][:, kc, :], rhs_fn(kc), kc == 0, kc == kcn - 1,
                       [("w", sl)] + rkey_fn(kc), [pskey])

            def setup():
                P.op("sp", lambda e: e.dma_start(out=ident[:], in_=cident[:, :]), writes=["ident"], dma=True)
                P.op("sp", lambda e: e.dma_start(out=maskf[:], in_=cmask[:, :]), writes=["maskf"], dma=True)
                P.op("sp", lambda e: e.dma_start(out=vstA[:C.RA, :128], in_=vecA[:, :]), writes=[("tx", 0)], dma=True)
                P.op("sp", lambda e: e.dma_start(out=vstB[:C.RB, :128], in_=vecB[:, :]), writes=[("tx", 1)], dma=True)
                copy_op("dve", identb[:], ident[:], ["ident"], ["identb"])
                P.op("dve", lambda e: e.memset(onesb[:], 1.0), writes=["onesb"])
                P.op("dve", lambda e: e.memset(onesf[:], 1.0), writes=["onesf"])
                for c in range(NSEGMAX):
                    copy_op("dve", maskr[:, c, :], maskf[:], ["maskf"], ["maskr"])
                P.op("dve", lambda e: e.memset(S[:].rearrange("p a b c -> p (a b c)"), 0.0),
                     writes=[("S", s_, h) for s_ in range(C.NSS) for h in range(NH)])
                P.op("dve", lambda e: e.memset(convst[:].rearrange("p a b c -> p (a b c)"), 0.0),
                     writes=[("cs", s_) for s_ in range(C.NSS)])
                tr(pbank[0][:, :C.RA], vstA[:C.RA, :128], ident[:C.RA, :C.RA], [("tx", 0), "ident"], [("pb", 0)])
                copy_op("dve", wn[:].rearrange("p a b -> p (a b)"), pbank[0][:, :C.RA], [("pb", 0)], ["wn"])
                tr(pbank[1][:, :C.RB], vstB[:C.RB, :128], ident[:C.RB, :C.RB], [("tx", 1), "ident"], [("pb", 1)])
                copy_op("dve", vB[:], pbank[1][:, :C.RB], [("pb", 1)], ["vB"])
                o0 = 3 * NCG
                tt(lbv[:, 1, :], vB[:, o0:o0 + NH], vB[:, o0 + NH:o0 + 2 * NH], ALU.subtract, ["vB"], ["lbv"])
                act(lbv[:, 0, :], lbv[:, 1, :], AF.Sigmoid, ["lbv"], ["lbv"])
                ts(lbv[:, 1, :], lbv[:, 0, :], -1.0, 1.0, ALU.mult, ALU.add, ["lbv"], ["lbv"])
                ts(lbv[:, 2, :], lbv[:, 0, :], -1.0, None, ALU.add, None, ["lbv"], ["lbv"])

            def cw(r, j):
                return vB[:, r * NCG + j:r * NCG + j + 1]

            hgw = vB[:, 3 * NCG + 2 * NH:3 * NCG + 2 * NH + 1]

            xpf = {"n": 0, "k0": 0}

            def prefetch_x(nxt, nmax=2, k0=0):
                if nxt is None:
                    return
                xsrc, T = nxt
                nsub = -(-T // 128)
                n = min(nmax, nsub)
                for sub in range(n):
                    rows = min(128, T - sub * 128)
                    kk = (sub + k0) % 2
                    st = xin(kk)
                    P.op("sp", (lambda e, st=st, sub=sub, rows=rows, xsrc=xsrc: e.dma_start(out=st[:rows, :], in_=xsrc[sub * 128:sub * 128 + rows, :])),
                         writes=xink(kk), dma=True)
                xpf["n"] = n
                xpf["k0"] = k0

            def load_tile(xsrc, T, widx):
                nsub = -(-T // 128)
                pss = pbank[4]
                for sub in range(nsub):
                    rows = min(128, T - sub * 128)
                    k = (sub + xpf["k0"]) % 2
                    st = xin(k)
                    cols = slice(sub * 128, sub * 128 + rows)
                    if sub >= xpf["n"]:
                        P.op("sp", (lambda e, st=st, sub=sub, rows=rows: e.dma_start(out=st[:rows, :], in_=xsrc[sub * 128:sub * 128 + rows, :])),
                             writes=xink(k), dma=True)
                    for c0 in range(0, KC, 4):
                        nb = min(4, KC - c0)
                        bi = nextbank()
                        bv = pv4(bi)
                        for i in range(nb):
                            c = c0 + i
                            tr(bv[:, i, :rows], st[:rows, c * 128:(c + 1) * 128], ident[:rows, :rows],
                               xink(k) + ["ident"], [("pb", bi)])
                        copy_op("act", xT[:, c0:c0 + nb, cols], bv[:, :nb, :rows],
                                [("pb", bi)], [("xT", c0 + i) for i in range(nb)] + [("xTs", c0 + i, sub) for i in range(nb)])
                    P.op("act", (lambda e, st=st, rows=rows, sub=sub: e.activation(out=st[:rows, :], in_=st[:rows, :], func=AF.Square,
                                                                                  accum_out=ssq[:rows, sub:sub + 1])),
                         xink(k), xink(k) + [("ssq", sub)])
                    act(ssq[:rows, 4 + sub:5 + sub], ssq[:rows, sub:sub + 1], AF.Ln, [("ssq", sub)], [("ssq2", sub)],
                        bias=EPSC[:rows, 0:1], scale=1.0 / D)
                    act(ssq[:rows, 4 + sub:5 + sub], ssq[:rows, 4 + sub:5 + sub], AF.Exp, [("ssq2", sub)], [("ssq2", sub)], scale=-0.5)
                    ts(rbc[:rows, :], onesf[:rows, :], ssq[:rows, 4 + sub:5 + sub], None, ALU.mult, None, [("ssq2", sub), "onesf"], ["rbc"])
                    tr(pss[:, cols], rbc[:rows, :], ident[:rows, :rows], ["rbc", "ident"], [("pb", 4)])
                    for c in range(KC):
                        stt(hT[:, c, cols], xT[:, c, cols], wn[:, widx, c:c + 1], pss[:, cols], ALU.mult, ALU.mult,
                            [("xTs", c, sub), "wn", ("pb", 4)], [("h", c)])

            def norm(T, widx, to_h):
                pss = pbank[4]
                for c in range(KC):
                    sqb = sq[c % 2]
                    act(sqb[:, :T], xT[:, c, :T], AF.Square, [("xT", c)], [("sq", c % 2)])
                    mm(pss[:, :T], onesb[:], sqb[:, :T], c == 0, c == KC - 1, [("sq", c % 2), "onesb"], [("pb", 4)])
                act(tx[0][:, :T], pss[:, :T], AF.Ln, [("pb", 4)], [("tx", 0)], bias=EPSC[:, 0:1], scale=1.0 / D)
                act(tx[1][:, :T], tx[0][:, :T], AF.Exp, [("tx", 0)], [("tx", 1)], scale=-0.5)
                for c in range(KC):
                    if to_h:
                        stt(hT[:, c, :T], xT[:, c, :T], wn[:, widx, c:c + 1], tx[1][:, :T], ALU.mult, ALU.mult,
                            [("xT", c), "wn", ("tx", 1)], [("h", c)])
                    else:
                        stt(xT[:, c, :T], xT[:, c, :T], wn[:, widx, c:c + 1], tx[1][:, :T], ALU.mult, ALU.mult,
                            [("xT", c), "wn", ("tx", 1)], [("xT", c)])

            def h_rhs(T, lo=0):
                return (lambda kc: hT[:, kc, lo:T]), (lambda kc: [("h", kc)])

            def make_head(h, T, segs, state_only):
                nseg = len(segs)
                L = segs[0][1]
                assert all(s_[1] == L and s_[0] == i * L for i, s_ in enumerate(segs)) and nseg * L == T
                ref = L // 2 - 1
                base = 3 * NCG
                rf, rk = h_rhs(T)
                cols = {"q": base + h, "z": base + NH + h, "i": base + 2 * NH + h, "o": base + 3 * NH + h}
                par = h % 2
                vT = vTb[par][:, :T]
                kvT = ("vT", par)
                t = [tmp(i)[:, :T] for i in range(8)]
                tk = [tmpk(i) for i in range(8)]
                lb_c, oml_c, noml_c = lbv[:, 0, h:h + 1], lbv[:, 1, h:h + 1], lbv[:, 2, h:h + 1]
                ps = {}
                attv, snv = pv4(4), pv4(6)
                po_ = pbank[5]

                def proj(n, kc0=0, kc1=None):
                    if kc0 == 0:
                        ps[n] = (wnext(w_in, 0, KC, cols[n] * 128, tag=(n, h)), nextbank())
                    sl, bi = ps[n]
                    kc1_ = KC if kc1 is None else kc1
                    for kc in range(kc0, kc1_):
                        mm(pbank[bi][:, :T], wslot[sl][:, kc, :], rf(kc), kc == 0, kc == KC - 1,
                           [("w", sl)] + rk(kc), [("pb", bi)])

                def A1():
                    proj("z")

                def A2():
                    names = ("i",) if state_only else ("i", "q")
                    for n in names:
                        proj(n)
                    pz, pi = pbank[ps["z"][1]][:, :T], pbank[ps["i"][1]][:, :T]
                    kz, ki = ("pb", ps["z"][1]), ("pb", ps["i"][1])
                    act(t[0], pz, AF.Sigmoid, [kz], tk[0])
                    if not state_only:
                        pq = pbank[ps["q"][1]][:, :T]
                        kq = ("pb", ps["q"][1])
                        act(t[1], pq, AF.Sigmoid, [kq], tk[1])
                    P.op("act", lambda e: e.copy(out=vT, in_=pi), [ki], [kvT])
                    if not state_only:
                        tt(t[3], pq, t[1], ALU.mult, [kq] + tk[1], tk[3])

                def B():
                    ts(t[5], t[0], noml_c, oml_c, ALU.mult, ALU.add, tk[0] + ["lbv"], tk[5])
                    act(t[6], t[0], AF.Ln, tk[0] + ["lbv"], tk[6], bias=lb_c, scale=oml_c)
                    for (c0, L_, sid) in segs:
                        P.op("dve", (lambda e, c0=c0: e.tensor_tensor_scan(out=t[7][:, c0:c0 + L], data0=onesf[:, :L], data1=t[6][:, c0:c0 + L],
                                                                          initial=0.0, op0=ALU.mult, op1=ALU.add)),
                             tk[6] + ["onesf"], tk[7])
                    lc3 = t[7].rearrange("p (s l) -> p s l", l=L)
                    if not state_only:
                        ts(nref[:, :nseg], lc3[:, :, ref], -1.0, None, ALU.mult, None, tk[7], ["nref"])
                    act(dec[par][:, :nseg], lc3[:, :, L - 1], AF.Exp, tk[7], [("dec", par)])
                    for ci, (c0, L_, sid) in enumerate(segs):
                        act(t[0][:, c0:c0 + L], t[7][:, c0:c0 + L], AF.Exp, tk[7], tk[0],
                            bias=t[7][:, c0 + L - 1:c0 + L], scale=-1.0)
                    tt(bft["kk"][:, :T], t[5], t[0], ALU.mult, tk[5] + tk[0], ["kk"])
                    if not state_only:
                        for ci, (c0, L_, sid) in enumerate(segs):
                            act(t[1][:, c0:c0 + L], t[7][:, c0:c0 + L], AF.Exp, tk[7] + ["nref"], tk[1],
                                bias=nref[:, ci:ci + 1], scale=1.0)
                        for ci, (c0, L_, sid) in enumerate(segs):
                            act(t[2][:, c0:c0 + L], t[7][:, c0:c0 + L], AF.Exp, tk[7], tk[2],
                                bias=t[7][:, c0 + ref:c0 + ref + 1], scale=-1.0)
                        act(t[6], t[7], AF.Exp, tk[7], tk[6])
                        tt(bft["qd"][:, :T], t[3], t[1], ALU.mult, tk[3] + tk[1], ["qd"])
                        tt(bft["kd"][:, :T], t[5], t[2], ALU.mult, tk[5] + tk[2], ["kd"])
                        tt(bft["qs"][:, :T], t[3], t[6], ALU.mult, tk[3] + tk[6], ["qs"])

                def C1():
                    k1, k2 = KC // 3, (2 * KC) // 3
                    for ci, (c0, L_, sid) in enumerate(segs):
                        tr(ptr[:L, ci, :], bft["kk"][:, c0:c0 + L], identb[:], ["kk", "identb"], ["ptr"])
                        tr(ptr[:L, 4 + ci, :], vT[:, c0:c0 + L], identb[:], [kvT, "identb"], ["ptr"])
                        if not state_only:
                            mm(attv[:L, ci, :L], bft["kd"][:, c0:c0 + L], bft["qd"][:, c0:c0 + L], True, True,
                               ["kd", "qd"], [("pb", 4)])
                    P.op("act", lambda e: e.copy(out=kktok[:L, :nseg, :], in_=ptr[:L, 0:nseg, :]), ["ptr"], ["kktok"])
                    P.op("act", lambda e: e.copy(out=vtok[:L, :nseg, :], in_=ptr[:L, 4:4 + nseg, :]), ["ptr"], ["vtok"])
                    if not state_only:
                        tt(attb[:L, :nseg, :L], attv[:L, :nseg, :L], maskr[:L, :nseg, :L], ALU.mult,
                           [("pb", 4), "maskr"], ["attb"])
                        proj("o", 0, k1)

                def C2():
                    k1, k2 = KC // 3, (2 * KC) // 3
                    for ci, (c0, L_, sid) in enumerate(segs):
                        mm(snv[:, ci, :], kktok[:L, ci, :], vtok[:L, ci, :], True, True, ["kktok", "vtok"], [("pb", 6)])
                    for ci, (c0, L_, sid) in enumerate(segs):
                        if not state_only:
                            P.op("act", (lambda e, ci=ci, sid=sid: e.copy(out=Sb[:, ci, :], in_=S[:, sid, h, :])),
                                 [("S", sid, h)], [("Sb", ci)])
                        stt(S[:, sid, h, :], S[:, sid, h, :], dec[par][:, ci:ci + 1], snv[:, ci, :], ALU.mult, ALU.add,
                            [("S", sid, h), ("dec", par), ("pb", 6)], [("S", sid, h)])
                    if state_only:
                        return
                    proj("o", k1, k2)
                    for ci, (c0, L_, sid) in enumerate(segs):
                        mm(po_[:, c0:c0 + L], vtok[:L, ci, :], attb[:L, ci, :L], True, False, ["vtok", "attb"], [("pb", 5)])
                        mm(po_[:, c0:c0 + L], Sb[:, ci, :], bft["qs"][:, c0:c0 + L], False, True, [("Sb", ci), "qs"], [("pb", 5)])
                    c0_, c1_ = tx[0][:, :T], tx[1][:, :T]
                    P.op("act", lambda e: e.copy(out=c0_, in_=po_[:, :T]), [("pb", 5)], [("tx", 0)])
                    act(bft["osq"][:, :T], po_[:, :T], AF.Square, [("pb", 5)], [("sq", 0)])
                    proj("o", k2, KC)
                    pog = pbank[ps["o"][1]][:, :T]
                    kog = ("pb", ps["o"][1])
                    mm(pbank[4][:, :T], onesb[:], bft["osq"][:, :T], True, True, [("sq", 0), "onesb"], [("pb", 4)])
                    act(t[4], pog, AF.Sigmoid, [kog], tk[4])
                    tt(t[4], pog, t[4], ALU.mult, [kog] + tk[4], tk[4])
                    act(c1_, pbank[4][:, :T], AF.Ln, [("pb", 4)], [("tx", 1)], bias=EPSC[:, 0:1], scale=1.0 / 128)
                    act(c1_, c1_, AF.Exp, [("tx", 1)], [("tx", 1)], scale=-0.5)
                    stt(c0_, c0_, hgw, c1_, ALU.mult, ALU.mult, [("tx", 0), ("tx", 1), "vB"], [("tx", 0)])
                    tt(yT[:, NCG + h, :T], c0_, t[4], ALU.mult, [("tx", 0)] + tk[4], [("y", NCG + h)])

                return A1, A2, B, C1, C2

            def heads(T, segs, state_only):
                st = [make_head(h, T, segs, state_only) for h in range(NH)]
                st[0][0]()
                st[0][1]()
                st[0][2]()
                for h in range(NH):
                    nx = st[h + 1] if h + 1 < NH else None
                    if state_only:
                        if nx:
                            nx[0]()
                        st[h][3]()
                        if nx:
                            nx[1]()
                        st[h][4]()
                    else:
                        if nx:
                            nx[0]()
                            nx[1]()
                        st[h][3]()
                        st[h][4]()
                    if nx:
                        nx[2]()

            def convgroup(j, T, cseqs, prefix_cu=False):
                rf, rk = h_rhs(T)
                ps = {}
                sls = {}
                for n, col in (("b", j), ("c", NCG + j), ("u", 2 * NCG + j)):
                    sl = wnext(w_in, 0, KC, col * 128)
                    bi = nextbank()
                    mm_chunk(pbank[bi][:, :T], ("pb", bi), sl, KC, rf, rk)
                    ps[n] = bi
                    if prefix_cu and n in ("c", "u"):
                        i_ = 0 if n == "c" else 1
                        for kc in range(KC):
                            mm(pbank[5][0:2, i_ * 128:(i_ + 1) * 128], hlast[:, kc, :], wslot[sl][:, kc, :], kc == 0, kc == KC - 1,
                               [("w", sl), "hlast"], [("pb", 5)])
                pb_, pc_, pu_ = (pbank[ps[n]] for n in ("b", "c", "u"))
                if prefix_cu:
                    p5 = pbank[5]
                    P.op("act", lambda e: e.copy(out=cst[0:2, :], in_=p5[0:2, 128:256]), [("pb", 5)], ["cst"])
                    tt(cst[0:2, :], p5[0:2, 0:128], cst[0:2, :], ALU.mult, [("pb", 5), "cst"], ["cst"])
                    tr(p5[:, 256:258], cst[0:2, :], ident[0:2, 0:2], ["cst", "ident"], [("pb", 5)])
                    P.op("act", lambda e: e.copy(out=convst[:, 0, :, j], in_=p5[:, 256:258]), [("pb", 5)], [("cs", 0)])
                P.op("act", lambda e: e.copy(out=tx[0][:, :T], in_=pu_[:, :T]), [("pb", ps["u"])], [("tx", 0)])
                for qi, (c0, n, cid) in enumerate(cseqs):
                    b0 = c0 + 2 * qi
                    P.op("act", (lambda e, b0=b0, cid=cid: e.copy(out=cub[:, b0:b0 + 2], in_=convst[:, cid, :, j])),
                         [("cs", cid)], ["cub"])
                    tt(cub[:, b0 + 2:b0 + 2 + n], pc_[:, c0:c0 + n], tx[0][:, c0:c0 + n], ALU.mult,
                       [("pb", ps["c"]), ("tx", 0)], ["cub"])
                    ts(tx[1][:, c0:c0 + n], cub[:, b0 + 2:b0 + 2 + n], cw(2, j), None, ALU.mult, None, ["cub", "vB"], [("tx", 1)])
                    stt(tx[1][:, c0:c0 + n], cub[:, b0 + 1:b0 + 1 + n], cw(1, j), tx[1][:, c0:c0 + n], ALU.mult, ALU.add,
                        ["cub", "vB", ("tx", 1)], [("tx", 1)])
                    stt(tx[1][:, c0:c0 + n], cub[:, b0:b0 + n], cw(0, j), tx[1][:, c0:c0 + n], ALU.mult, ALU.add,
                        ["cub", "vB", ("tx", 1)], [("tx", 1)])
                    tt(yT[:, j, c0:c0 + n], tx[1][:, c0:c0 + n], pb_[:, c0:c0 + n], ALU.mult,
                       [("tx", 1), ("pb", ps["b"])], [("y", j)])
                    P.op("act", (lambda e, b0=b0, n=n, cid=cid: e.copy(out=convst[:, cid, :, j], in_=cub[:, b0 + n:b0 + n + 2])),
                         ["cub"], [("cs", cid)])

            def conv_state_only(j, T, cid):
                rf = lambda kc: hT[:, kc, T - 2:T]
                rk = lambda kc: [("h", kc)]
                ps = {}
                for n, col in (("c", NCG + j), ("u", 2 * NCG + j)):
                    sl = wnext(w_in, 0, KC, col * 128)
                    bi = nextbank()
                    mm_chunk(pbank[bi][:, :2], ("pb", bi), sl, KC, rf, rk)
                    ps[n] = bi
                P.op("act", lambda e: e.copy(out=tx[0][:, :2], in_=pbank[ps["u"]][:, :2]), [("pb", ps["u"])], [("tx", 0)])
                tt(convst[:, cid, :, j], pbank[ps["c"]][:, :2], tx[0][:, :2], ALU.mult, [("pb", ps["c"]), ("tx", 0)], [("cs", cid)])

            def wout(T):
                for oc in range(KC):
                    sl = wnext(w_out, 0, KC, oc * 128)
                    bi = nextbank()
                    mm_chunk(pbank[bi][:, :T], ("pb", bi), sl, KC, (lambda kc: yT[:, kc, :T]), (lambda kc: [("y", kc)]))
                    tt(xT[:, oc, :T], pbank[bi][:, :T], xT[:, oc, :T], ALU.add, [("pb", bi), ("xT", oc)], [("xT", oc)])

            def ffn(T, ride=False):
                rf, rk = h_rhs(T)
                pG, pU, pD = pbank[4][:, 0:TSR], pbank[5][:, 0:TSR], pbank[6][:, 0:TSR]
                stx = cub[:, 0:2 * TSR].rearrange("p (k n) -> p k n", n=TSR)

                def smm(ps_ap, pskey, sl, kcn, rhs_fn, rkey_fn):
                    for kc in range(kcn):
                        mm(ps_ap, wslot[sl][:, kc, :], rhs_fn(kc), kc == 0, kc == kcn - 1,
                           [("w", sl)] + rkey_fn(kc), [pskey])

                for g0 in range(0, FC, C.J):
                    jn = min(C.J, FC - g0)
                    for jj in range(jn):
                        f = g0 + jj
                        k = jj % 2
                        sl = wnext(w_gate, 0, KC, f * 128)
                        bg = nextbank()
                        mm_chunk(pbank[bg][:, :T], ("pb", bg), sl, KC, rf, rk)
                        if ride:
                            smm(pG, ("pb", 4), sl, KC, (lambda kc: h2s[:, kc, :]), (lambda kc: [("hs", kc)]))
                        sl = wnext(w_up, 0, KC, f * 128)
                        bu = nextbank()
                        mm_chunk(pbank[bu][:, :T], ("pb", bu), sl, KC, rf, rk)
                        if ride:
                            smm(pU, ("pb", 5), sl, KC, (lambda kc: h2s[:, kc, :]), (lambda kc: [("hs", kc)]))
                        act(tx[k][:, :T], pbank[bg][:, :T], AF.Sigmoid, [("pb", bg)], [("tx", k)])
                        tt(tx[k][:, :T], pbank[bg][:, :T], tx[k][:, :T], ALU.mult, [("pb", bg), ("tx", k)], [("tx", k)])
                        tt(yT[:, jj, :T], tx[k][:, :T], pbank[bu][:, :T], ALU.mult, [("tx", k), ("pb", bu)], [("y", jj)])
                        if ride:
                            act(stx[:, k, :], pG, AF.Sigmoid, [("pb", 4)], [("stx", k)])
                            tt(stx[:, k, :], pG, stx[:, k, :], ALU.mult, [("pb", 4), ("stx", k)], [("stx", k)])
                            tt(aTs[:, jj, :], stx[:, k, :], pU, ALU.mult, [("stx", k), ("pb", 5)], [("as", jj)])
                    for oc in range(KC):
                        sl = wnext(w_down, g0 * 128, jn, oc * 128)
                        bi = nextbank()
                        mm_chunk(pbank[bi][:, :T], ("pb", bi), sl, jn, (lambda kc: yT[:, kc, :T]), (lambda kc: [("y", kc)]))
                        if ride:
                            smm(pD, ("pb", 6), sl, jn, (lambda kc: aTs[:, kc, :]), (lambda kc: [("as", kc)]))
                        tt(xT[:, oc, :T], pbank[bi][:, :T], xT[:, oc, :T], ALU.add, [("pb", bi), ("xT", oc)], [("xT", oc)])
                        if ride:
                            tt(x1s[:, oc, :], pD, x1s[:, oc, :], ALU.add, [("pb", 6), ("xs", oc)], [("xs", oc)])

            def store_y(ydst, T):
                norm(T, 2, False)
                nsub = -(-T // 128)
                for sub in range(nsub):
                    rows = min(128, T - sub * 128)
                    k = sub % 2
                    st = ost(k)
                    for c0 in range(0, KC, 4):
                        nb = min(4, KC - c0)
                        bi = nextbank()
                        bv = pv4(bi)
                        for i in range(nb):
                            c = c0 + i
                            tr(bv[:rows, i, :], xT[:, c, sub * 128:sub * 128 + rows], ident[:], [("xT", c), "ident"], [("pb", bi)])
                        copy_op(evac_eng(), st[:rows, c0 * 128:(c0 + nb) * 128].rearrange("p (a b) -> p a b", a=nb),
                                bv[:rows, :nb, :], [("pb", bi)], ostk(k))
                    P.op("sp", (lambda e, st=st, sub=sub, rows=rows: e.dma_start(out=ydst[sub * 128:sub * 128 + rows, :], in_=st[:rows, :])),
                         reads=ostk(k), dma=True, must_wait=True)

            def store_states(sid, hg_dst, conv_dst):
                P.op("sp", lambda e: e.dma_start(out=hg_dst.rearrange("h k v -> k h v"), in_=S[:, sid, :, :]),
                     reads=[("S", sid, h) for h in range(NH)], dma=True, must_wait=True)
                bi = nextbank()
                tr(pbank[bi][:2 * NCG, :128], convst[:, sid, :, :].rearrange("p a b -> p (a b)"), ident[:],
                   [("cs", sid), "ident"], [("pb", bi)])
                copy_op("dve", cst[:2 * NCG, :], pbank[bi][:2 * NCG, :128], [("pb", bi)], ["cst"])
                P.op("sp", lambda e: e.dma_start(out=conv_dst, in_=cst[:2 * NCG, :]), reads=["cst"], dma=True, must_wait=True)

            def load_states(sid, hg_src, conv_src):
                P.op("sp", lambda e: e.dma_start(out=S[:, sid, :, :], in_=hg_src.rearrange("h k v -> k h v")),
                     writes=[("S", sid, h) for h in range(NH)], dma=True)
                P.op("sp", lambda e: e.dma_start(out=cst[:2 * NCG, :], in_=conv_src), writes=["cst"], dma=True)
                bi = nextbank()
                tr(pbank[bi][:, :2 * NCG], cst[:2 * NCG, :], ident[:2 * NCG, :2 * NCG], ["cst", "ident"], [("pb", bi)])
                copy_op("dve", convst[:, sid, :, :].rearrange("p a b -> p (a b)"), pbank[bi][:, :2 * NCG], [("pb", bi)], [("cs", sid)])

            def main_tile(mt, xsrc, ydst, T, segs, cseqs, ride=False, nxt=None):
                mt_state["mt"] = mt
                mt_state["pt"] = None
                mt_state["cidx"] = 0
                mt_state["nowb"] = False
                load_tile(xsrc, T, 0)
                xpf["n"] = 0
                xpf["k0"] = 0
                heads(T, segs, False)
                for j in range(NCG):
                    convgroup(j, T, cseqs, prefix_cu=(mt == 0))
                wout(T)
                norm(T, 1, True)
                ffn(T, ride)
                prefetch_x(nxt)
                store_y(ydst, T)

            def sample_front(mt, xsrc, T, segs, cseqs, nxt=None):
                mt_state["mt"] = mt
                mt_state["pt"] = None
                mt_state["cidx"] = 0
                mt_state["nowb"] = True
                load_tile(xsrc, T, 0)
                xpf["n"] = 0
                xpf["k0"] = 0
                heads(T, segs, False)
                for j in range(NCG):
                    convgroup(j, T, cseqs)
                wout(T)
                norm(T, 1, True)
                mt_state["nowb"] = False
                mt_state["ns"] = 3
                P.op("dve", lambda e: e.tensor_copy(out=x1s[:, :KC, :], in_=xT[:, :, :T]),
                     [("xT", c) for c in range(KC)], [("xs", c) for c in range(KC)] + [("w", 3)])
                P.op("dve", lambda e: e.tensor_copy(out=h2s[:, :KC, :], in_=hT[:, :, :T]),
                     [("h", c) for c in range(KC)], [("hs", c) for c in range(KC)] + [("as", c) for c in range(C.KS)] + [("w", 4)])
                prefetch_x(nxt)

            def sample_back(ydst, T):
                P.op("dve", lambda e: e.tensor_copy(out=xT[:, :, :T], in_=x1s[:, :KC, :]),
                     [("xs", c) for c in range(KC)], [("xT", c) for c in range(KC)])
                store_y(ydst, T)

            def prefix_tile(pt, xsrc, T, segs, last, nxt=None):
                mt_state["mt"] = None
                mt_state["pt"] = pt
                load_tile(xsrc, T, 0)
                xpf["n"] = 0
                xpf["k0"] = 0
                heads(T, segs, True)
                if C.TOFF == 0 and 16 * TT * 2 >= D * 4:
                    prefetch_x(nxt, nmax=1, k0=1)
                if last:
                    copy_op("dve", hlast[:], hT[:, :, T - 2:T], [("h", c) for c in range(KC)], ["hlast"])

            setup()
            pseg = [(i * 128, 128, 0) for i in range(TT // 128)]
            for t_ in range(C.NPRE):
                nx_ = (xpre[(t_ + 1) * TT:(t_ + 2) * TT, :], TT) if t_ + 1 < C.NPRE else (xmain[0:TT, :], TT)
                prefix_tile(t_, xpre[t_ * TT:(t_ + 1) * TT, :], TT, pseg, t_ == C.NPRE - 1, nxt=nx_)
            TS = C.NSS * C.LS
            sseg = [(i * C.LS, C.LS, i) for i in range(C.NSS)]
            RIDE = (NW >= 5 and C.NPT >= 2 and TS == TSR) and RIDE_ON
            for t_ in range(C.NPT):
                last = (t_ == C.NPT - 1)
                if last and RIDE:
                    P.op("sp", lambda e: e.dma_start(out=sspill.rearrange("h k v -> k h v"), in_=S[:, 0, :, :]),
                         reads=[("S", 0, h) for h in range(NH)], writes=["sspill"], dma=True)
                    copy_op("dve", cbak[:], convst[:, 0, :, :], [("cs", 0)], ["cbak"])
                    for s_ in range(C.NSS):
                        load_states(s_, shg[s_], cconv[s_])
                    sample_front(t_, xsam, TS, sseg, sseg, nxt=(xmain[t_ * TT:(t_ + 1) * TT, :], TT))
                    for s_ in range(C.NSS):
                        store_states(s_, hg_s[s_], conv_s[s_])
                    P.op("sp", lambda e: e.dma_start(out=S[:, 0, :, :], in_=sspill.rearrange("h k v -> k h v")),
                         reads=["sspill"], writes=[("S", 0, h) for h in range(NH)], dma=True)
                    copy_op("dve", convst[:, 0, :, :], cbak[:], ["cbak"], [("cs", 0)])
                if t_ + 1 < C.NPT:
                    nxt_ = (xsam, TS) if (t_ + 1 == C.NPT - 1 and RIDE) else (xmain[(t_ + 1) * TT:(t_ + 2) * TT, :], TT)
                else:
                    nxt_ = None if RIDE else (xsam, TS)
                main_tile(t_, xmain[t_ * TT:(t_ + 1) * TT, :], y_main[t_ * TT:(t_ + 1) * TT, :], TT, pseg, [(0, TT, 0)],
                          ride=(last and RIDE and RIDE_FFN), nxt=nxt_)
            store_states(0, hg_p, conv_p[:, :])
            if RIDE:
                sample_back(y_s, TS)
            else:
                for s_ in range(C.NSS):
                    load_states(s_, shg[s_], cconv[s_])
                main_tile(C.NPT, xsam, y_s, TS, sseg, sseg)
                for s_ in range(C.NSS):
                    store_states(s_, hg_s[s_], conv_s[s_])

        Pd = Prog(dry=True)
        plan = []
        emit_all(Pd, plan)
        rr = 0
        last_in = {}
        for k_, ent_ in enumerate(plan):
            sl_ = rr % ent_[6]
            rr += 1
            slot_of.append(sl_)
            prev_occ.append(last_in.get(sl_))
            last_in[sl_] = k_
        P = Prog(dry=False, n_dma_sems=ndma)
        P.op("dve", lambda e: e.memset(EPSC[:], EPS), writes=["epsc"])
        emit_all(P, plan)
        state["nops"] = len(P.ops)
        P.emit(block, sems)
    return nc, state


_CACHE = {}


def _consts():
    ident = np.eye(128, dtype=np.float32)
    s = np.arange(128)[:, None]
    t = np.arange(128)[None, :]
    mask = (t >= s).astype(np.float32)
    return ident, mask


def run_cfg(cfg, inputs, n_cores=8, trace=False):
    C = cfg
    key = (C.D, C.DC, C.NH, C.DFF, C.TT, C.NPT, C.NPRE, C.NSS, C.LS, C.NW)
    if key not in _CACHE:
        _CACHE[key] = build_program(C)
    nc, st = _CACHE[key]
    f = lambda a: np.ascontiguousarray(np.asarray(a, dtype=np.float32))
    xp, xs = f(inputs["x_prompt"]), f(inputs["x_sample"])
    B, SEQ, D = xp.shape
    half = SEQ // 2
    assert half == C.NPT * C.TT == C.NPRE * C.TT and n_cores == 2 * B
    assert xs.shape[0] == C.NSS * n_cores and xs.shape[1] == C.LS
    cache_conv, state_hgrn = f(inputs["cache_conv"]), f(inputs["state_hgrn"])
    w_in, w_out = f(inputs["w_in"][0]), f(inputs["w_out"][0])
    w_gate, w_up, w_down = f(inputs["w_gate"][0]), f(inputs["w_up"][0]), f(inputs["w_down"][0])
    vecA = np.concatenate([f(inputs["norm_mix"][0]).reshape(C.KC, 128), f(inputs["norm_ffn"][0]).reshape(C.KC, 128),
                           f(inputs["norm_final"]).reshape(C.KC, 128)], axis=0)
    vecB = np.concatenate([f(inputs["conv_w"][0]).reshape(3 * C.NCG, 128), f(inputs["lb_logits"]).reshape(2 * C.NH, 128),
                           f(inputs["hg_norm"][0]).reshape(1, 128)], axis=0)
    ident, mask = _consts()
    zeros_pre = np.zeros((half, D), np.float32)
    in_maps = []
    for c in range(n_cores):
        s, hf = c // 2, c % 2
        in_maps.append({
            "xmain": np.ascontiguousarray(xp[s, hf * half:(hf + 1) * half]),
            "xpre": np.ascontiguousarray(xp[s, :half]) if hf == 1 else zeros_pre,
            "xsam": np.ascontiguousarray(xs[C.NSS * c:C.NSS * (c + 1)].reshape(C.NSS * C.LS, D)),
            "cconv": np.ascontiguousarray(cache_conv[0, C.NSS * c:C.NSS * (c + 1)].reshape(C.NSS, 2 * C.NCG, 128)),
            "shg": np.ascontiguousarray(state_hgrn[0, C.NSS * c:C.NSS * (c + 1)]),
            "w_in": w_in, "w_out": w_out, "w_gate": w_gate, "w_up": w_up, "w_down": w_down,
            "vecA": vecA, "vecB": vecB, "cident": ident, "cmask": mask,
        })
    res = run_bass_kernel_spmd(nc, in_maps, core_ids=list(range(n_cores)), trace=trace)
    R = res.results
    y_prompt = np.empty((B, SEQ, D), np.float32)
    y_sample = np.empty(xs.shape, np.float32)
    ncp = np.empty((1, B, 2, C.DC), np.float32)
    nhp = np.empty((1, B, C.NH, 128, 128), np.float32)
    ncs = np.empty((1, xs.shape[0], 2, C.DC), np.float32)
    nhs = np.empty((1, xs.shape[0], C.NH, 128, 128), np.float32)
    for c in range(n_cores):
        s, hf = c // 2, c % 2
        r = R[c]
        y_prompt[s, hf * half:(hf + 1) * half] = r["y_main"]
        y_sample[C.NSS * c:C.NSS * (c + 1)] = r["y_s"].reshape(C.NSS, C.LS, D)
        if hf == 1:
            ncp[0, s] = r["conv_p"].reshape(2, C.DC)
            nhp[0, s] = r["hg_p"]
        ncs[0, C.NSS * c:C.NSS * (c + 1)] = r["conv_s"].reshape(C.NSS, 2, C.DC)
        nhs[0, C.NSS * c:C.NSS * (c + 1)] = r["hg_s"]
    out = (y_prompt, y_sample, ncp, nhp, ncs, nhs)
    if trace:
        return out, res
    return out


def kernel(**inputs):
    return run_cfg(Cfg(), inputs)
```
